# Optimizing a Trainium2 kernel written in Bass

```python
import jax, jax.numpy as jnp
from jax import lax
import numpy as np

D_MODEL = 1024
BATCH = 16
SEQ = 2048
DEPTH = 4

N_MIXERS = 2
HEAD_DIM = 64
ROPE_THETA = 10000.0
NORM_EPS = 1e-6

NSA_HEADS = 16
NSA_KV_HEADS = 4
CMP_LEN = 32
CMP_STRIDE = 16
CMP_HIDDEN = 256
SLC_LEN = 64
SLC_TOPK = 16
WIN_LEN = 512
FORCE_BONUS = 1e4
SLC_Q_CHUNK = 16
NSA_IN_DIM = NSA_HEADS * HEAD_DIM + 6 * NSA_KV_HEADS * HEAD_DIM + 3 * NSA_HEADS

SWA_HEADS = 16
SWA_KV_HEADS = 2
SWA_WINDOW = 128
SWA_IN_DIM = SWA_HEADS * HEAD_DIM + 2 * SWA_KV_HEADS * HEAD_DIM

Q_BLOCK = 128

D_FF = 2816
CONV_WIDTH = 3

N_A = (DEPTH + 1) // 2
N_B = DEPTH // 2

kernel_name = 'hybrid_nsa_swa_sink_convffn_trunk'


def rms_norm(x, gain):
    xf = x.astype(jnp.float32)
    y = xf * lax.rsqrt(jnp.mean(xf * xf, axis=-1, keepdims=True) + NORM_EPS)
    return (y * gain.astype(jnp.float32)).astype(x.dtype)


def rope_tables(seq):
    pos = jnp.arange(seq, dtype=jnp.float32)
    inv = ROPE_THETA ** (-jnp.arange(0, HEAD_DIM, 2, dtype=jnp.float32) / HEAD_DIM)
    ang = pos[:, None] * inv[None, :]
    return jnp.cos(ang), jnp.sin(ang)


def apply_rope(x, cos, sin):
    c = cos[None, :, None, :].astype(x.dtype)
    s = sin[None, :, None, :].astype(x.dtype)
    x1, x2 = jnp.split(x, 2, axis=-1)
    return jnp.concatenate([x1 * c - x2 * s, x2 * c + x1 * s], axis=-1)


def to_q_groups(q, n_groups):
    b, s, h, d = q.shape
    return q.reshape(b, s, n_groups, h // n_groups, d).transpose(0, 2, 3, 1, 4)


def from_q_groups(o):
    b, g, r, s, d = o.shape
    return o.transpose(0, 3, 1, 2, 4).reshape(b, s, g * r * d)


def masked_softmax(s, mask):
    s = jnp.where(mask, s, -jnp.inf)
    m = jnp.max(s, axis=-1, keepdims=True)
    m = jnp.where(jnp.isfinite(m), m, 0.0)
    p = jnp.exp(s - m)
    d = jnp.sum(p, axis=-1, keepdims=True)
    return p / jnp.where(d > 0.0, d, 1.0)


def banded_attention(q, k, v, window, sinks=None):
    b, g, r, seq, hd = q.shape
    nqb = seq // Q_BLOCK
    span = Q_BLOCK + window
    pad = ((0, 0), (0, 0), (window, 0), (0, 0))
    k_pad = jnp.pad(k, pad)
    v_pad = jnp.pad(v, pad)
    qb = q.reshape(b, g, r, nqb, Q_BLOCK, hd).transpose(3, 0, 1, 2, 4, 5)
    scale = hd ** -0.5

    def block(args):
        blk, q_blk = args
        start = blk * Q_BLOCK
        kb = lax.dynamic_slice_in_dim(k_pad, start, span, axis=2)
        vb = lax.dynamic_slice_in_dim(v_pad, start, span, axis=2)
        qpos = start + jnp.arange(Q_BLOCK)
        kpos = start - window + jnp.arange(span)
        rel = qpos[:, None] - kpos[None, :]
        mask = (rel >= 0) & (rel < window) & (kpos[None, :] >= 0)
        s = jnp.einsum('bgrqd,bgkd->bgrqk', q_blk, kb).astype(jnp.float32) * scale
        if sinks is None:
            p = masked_softmax(s, mask)
        else:
            s = jnp.where(mask, s, -jnp.inf)
            sink = sinks.astype(jnp.float32)[None, :, :, None, None]
            m = jnp.maximum(jnp.max(s, axis=-1, keepdims=True), sink)
            p = jnp.exp(s - m)
            p = p / (jnp.sum(p, axis=-1, keepdims=True) + jnp.exp(sink - m))
        return jnp.einsum('bgrqk,bgkd->bgrqd', p.astype(vb.dtype), vb)

    o = lax.map(block, (jnp.arange(nqb), qb))
    return o.transpose(1, 2, 3, 0, 4, 5).reshape(b, g, r, seq, hd)


def nsa_mixer(h, cos, sin, w_in, cmp_pe, cmp_w1, cmp_w2, gate_b, w_o):
    b, seq, _ = h.shape
    g, r, hd = NSA_KV_HEADS, NSA_HEADS // NSA_KV_HEADS, HEAD_DIM
    sizes = [NSA_HEADS * hd] + [g * hd] * 6 + [3 * NSA_HEADS]
    splits = np.cumsum(sizes)[:-1].tolist()
    q, kc, vc, ks, vs, kw, vw, gt = jnp.split(h @ w_in, splits, axis=-1)

    q = to_q_groups(apply_rope(q.reshape(b, seq, NSA_HEADS, hd), cos, sin), g)

    def kv(t, rotate):
        t = t.reshape(b, seq, g, hd)
        if rotate:
            t = apply_rope(t, cos, sin)
        return t.transpose(0, 2, 1, 3)

    kc, ks, kw = kv(kc, True), kv(ks, True), kv(kw, True)
    vc, vs, vw = kv(vc, False), kv(vs, False), kv(vw, False)
    pos = jnp.arange(seq)
    scale = hd ** -0.5

    n_cmp = (seq - CMP_LEN) // CMP_STRIDE + 1
    cstart = jnp.arange(n_cmp) * CMP_STRIDE
    cidx = cstart[:, None] + jnp.arange(CMP_LEN)[None, :]

    def compress(t, pe, w1, w2):
        blk = (t[:, :, cidx] + pe).reshape(b, g, n_cmp, CMP_LEN * hd)
        return jax.nn.gelu(blk @ w1) @ w2

    kc = compress(kc, cmp_pe[0], cmp_w1[0], cmp_w2[0])
    vc = compress(vc, cmp_pe[1], cmp_w1[1], cmp_w2[1])
    s_cmp = jnp.einsum('bgrqd,bgcd->bgrqc', q, kc).astype(jnp.float32) * scale
    cmask = (cstart + CMP_LEN - 1)[None, :] <= pos[:, None]
    p_cmp = masked_softmax(s_cmp, cmask)
    o_cmp = jnp.einsum('bgrqc,bgcd->bgrqd', p_cmp.astype(vc.dtype), vc)

    n_slc = seq // SLC_LEN
    top_n = min(SLC_TOPK, n_slc)
    sstart = jnp.arange(n_slc) * SLC_LEN
    overlap = ((cstart[:, None] <= sstart[None, :] + SLC_LEN - 1)
               & (cstart[:, None] + CMP_LEN - 1 >= sstart[None, :])).astype(jnp.float32)
    imp = jnp.einsum('bgrqc,cn->bgqn', p_cmp, overlap)
    cur = pos // SLC_LEN
    blk = jnp.arange(n_slc)
    forced = (blk[None, :] == 0) | (blk[None, :] == cur[:, None]) | (blk[None, :] == cur[:, None] - 1)
    causal = blk[None, :] <= cur[:, None]
    score = jnp.where(causal, imp + jnp.where(forced, FORCE_BONUS, 0.0), -1.0)
    top_score, sel = lax.top_k(score, top_n)
    sel_valid = top_score >= 0.0

    nqc = seq // SLC_Q_CHUNK
    q_ch = q.reshape(b, g, r, nqc, SLC_Q_CHUNK, hd).transpose(3, 0, 1, 2, 4, 5)
    sel_ch = sel.reshape(b, g, nqc, SLC_Q_CHUNK, top_n).transpose(2, 0, 1, 3, 4)
    val_ch = sel_valid.reshape(b, g, nqc, SLC_Q_CHUNK, top_n).transpose(2, 0, 1, 3, 4)
    pos_ch = pos.reshape(nqc, SLC_Q_CHUNK)
    gather_rows = jax.vmap(jax.vmap(lambda t, i: t[i]))
    n_keys = top_n * SLC_LEN

    def slc_block(args):
        q_c, sel_c, val_c, qpos = args
        tok = sel_c[..., None] * SLC_LEN + jnp.arange(SLC_LEN)
        tmask = val_c[..., None] & (tok <= qpos[None, None, :, None, None])
        tok = tok.reshape(b, g, SLC_Q_CHUNK, n_keys)
        tmask = tmask.reshape(b, g, SLC_Q_CHUNK, n_keys)
        kg = gather_rows(ks, tok)
        vg = gather_rows(vs, tok)
        s = jnp.einsum('bgrqd,bgqkd->bgrqk', q_c, kg).astype(jnp.float32) * scale
        p = masked_softmax(s, tmask[:, :, None])
        return jnp.einsum('bgrqk,bgqkd->bgrqd', p.astype(vg.dtype), vg)

    o_slc = lax.map(slc_block, (q_ch, sel_ch, val_ch, pos_ch))
    o_slc = o_slc.transpose(1, 2, 3, 0, 4, 5).reshape(b, g, r, seq, hd)

    o_win = banded_attention(q, kw, vw, WIN_LEN)

    gates = jax.nn.sigmoid((gt + gate_b).astype(jnp.float32))
    gates = gates.reshape(b, seq, g, r, 3).transpose(0, 2, 3, 1, 4).astype(q.dtype)
    o = gates[..., 0, None] * o_cmp + gates[..., 1, None] * o_slc + gates[..., 2, None] * o_win
    return from_q_groups(o) @ w_o


def swa_mixer(h, cos, sin, w_qkv, b_qkv, sinks, w_o, b_o):
    b, seq, _ = h.shape
    g, r, hd = SWA_KV_HEADS, SWA_HEADS // SWA_KV_HEADS, HEAD_DIM
    q, k, v = jnp.split(h @ w_qkv + b_qkv, [SWA_HEADS * hd, SWA_HEADS * hd + g * hd], axis=-1)
    q = to_q_groups(apply_rope(q.reshape(b, seq, SWA_HEADS, hd), cos, sin), g)
    k = apply_rope(k.reshape(b, seq, g, hd), cos, sin).transpose(0, 2, 1, 3)
    v = v.reshape(b, seq, g, hd).transpose(0, 2, 1, 3)
    o = banded_attention(q, k, v, SWA_WINDOW, sinks.reshape(g, r))
    return from_q_groups(o) @ w_o + b_o


def causal_depthwise_conv(a, w, bias):
    f = a.shape[-1]
    y = lax.conv_general_dilated(a, w.astype(a.dtype)[:, None, :], window_strides=(1,),
                                 padding=[(CONV_WIDTH - 1, 0)],
                                 dimension_numbers=('NWC', 'WIO', 'NWC'),
                                 feature_group_count=f)
    return y + bias.astype(a.dtype)


def conv_ffn(h, w_gu, conv_w, conv_b, w_down):
    a, u = jnp.split(h @ w_gu, 2, axis=-1)
    a = causal_depthwise_conv(a, conv_w, conv_b)
    return (jax.nn.silu(a) * u) @ w_down


def setup_inputs(seed: int = 0) -> dict:
    key = jax.random.key(seed)
    ks = jax.random.split(key, 20)
    f32 = jnp.float32

    def nrm(k, shape, scale):
        return jax.random.normal(k, shape, f32) * scale

    att_w = NSA_HEADS * HEAD_DIM
    swa_w = SWA_HEADS * HEAD_DIM
    return {
        'x': nrm(ks[0], (BATCH, SEQ, D_MODEL), 1.0),
        'nsa_w_in': nrm(ks[1], (N_A, D_MODEL, NSA_IN_DIM), D_MODEL ** -0.5),
        'nsa_cmp_pe': nrm(ks[2], (N_A, 2, CMP_LEN, HEAD_DIM), 0.02),
        'nsa_cmp_w1': nrm(ks[3], (N_A, 2, CMP_LEN * HEAD_DIM, CMP_HIDDEN), (CMP_LEN * HEAD_DIM) ** -0.5),
        'nsa_cmp_w2': nrm(ks[4], (N_A, 2, CMP_HIDDEN, HEAD_DIM), CMP_HIDDEN ** -0.5),
        'nsa_gate_b': nrm(ks[5], (N_A, 3 * NSA_HEADS), 0.1),
        'nsa_w_o': nrm(ks[6], (N_A, att_w, D_MODEL), att_w ** -0.5),
        'swa_w_qkv': nrm(ks[7], (N_B, D_MODEL, SWA_IN_DIM), D_MODEL ** -0.5),
        'swa_b_qkv': nrm(ks[8], (N_B, SWA_IN_DIM), 0.02),
        'swa_sinks': nrm(ks[9], (N_B, SWA_HEADS), 1.0),
        'swa_w_o': nrm(ks[10], (N_B, swa_w, D_MODEL), swa_w ** -0.5),
        'swa_b_o': nrm(ks[11], (N_B, D_MODEL), 0.02),
        'ffn_w_gu': nrm(ks[12], (DEPTH, D_MODEL, 2 * D_FF), D_MODEL ** -0.5),
        'ffn_conv_w': nrm(ks[13], (DEPTH, CONV_WIDTH, D_FF), CONV_WIDTH ** -0.5),
        'ffn_conv_b': nrm(ks[14], (DEPTH, D_FF), 0.02),
        'ffn_w_down': nrm(ks[15], (DEPTH, D_FF, D_MODEL), D_FF ** -0.5),
        'norm_mix': 1.0 + nrm(ks[16], (DEPTH, D_MODEL), 0.02),
        'norm_ffn': 1.0 + nrm(ks[17], (DEPTH, D_MODEL), 0.02),
        'norm_final': 1.0 + nrm(ks[18], (D_MODEL,), 0.02),
    }


def reference(x, nsa_w_in, nsa_cmp_pe, nsa_cmp_w1, nsa_cmp_w2, nsa_gate_b, nsa_w_o,
              swa_w_qkv, swa_b_qkv, swa_sinks, swa_w_o, swa_b_o,
              ffn_w_gu, ffn_conv_w, ffn_conv_b, ffn_w_down,
              norm_mix, norm_ffn, norm_final):
    cos, sin = rope_tables(x.shape[1])
    for i in range(DEPTH):
        j = i // N_MIXERS
        h = rms_norm(x, norm_mix[i])
        if i % N_MIXERS == 0:
            x = x + nsa_mixer(h, cos, sin, nsa_w_in[j], nsa_cmp_pe[j], nsa_cmp_w1[j],
                              nsa_cmp_w2[j], nsa_gate_b[j], nsa_w_o[j])
        else:
            x = x + swa_mixer(h, cos, sin, swa_w_qkv[j], swa_b_qkv[j], swa_sinks[j],
                              swa_w_o[j], swa_b_o[j])
        h = rms_norm(x, norm_ffn[i])
        x = x + conv_ffn(h, ffn_w_gu[i], ffn_conv_w[i], ffn_conv_b[i], ffn_w_down[i])
    return rms_norm(x, norm_final)
```

```python
import numpy as np
import concourse.bass as bass
import concourse.mybir as mybir
from concourse.bass_utils import run_bass_kernel_spmd

F32 = mybir.dt.float32
BF16 = mybir.dt.bfloat16
F32R = mybir.dt.float32r
AF = mybir.ActivationFunctionType
ALU = mybir.AluOpType
AX = mybir.AxisListType

S = 2048
D = 1024
KC = 8
NTB = 4
NTT = 16
DFF = 2816
NFC = 22
DEPTH = 4
BIG = 32768.0
SCALE = 0.125
EPS = 1e-6
NSLOT = 3
SLOTC = 4096
NPT = 6
GF = 4
STAGE = 99


class Rec:
    __slots__ = ("eng", "emit", "deps", "sig", "cnt", "fill", "ftarget")


class Ctr:
    def __init__(self, sem, name):
        self.sem = sem
        self.name = name
        self.count = 0


class Prog:
    ENG = ("pe", "act", "dve", "pool", "sp")

    def __init__(self, nc, esem):
        self.nc = nc
        self.esem = esem
        self.streams = {e: [] for e in self.ENG}
        self.last_w = {}
        self.readers = {}

    def op(self, eng, emit, reads=(), writes=(), fill=None):
        rec = Rec()
        rec.eng = eng
        rec.emit = emit
        rec.sig = False
        rec.cnt = None
        rec.fill = fill
        rec.ftarget = None
        deps = {}
        for k in reads:
            w = self.last_w.get(k)
            if w is not None:
                deps[id(w)] = w
        for k in writes:
            w = self.last_w.get(k)
            if w is not None:
                deps[id(w)] = w
            rd = self.readers.get(k)
            if rd:
                for r in rd.values():
                    deps[id(r)] = r
        for k in writes:
            self.last_w[k] = rec
            self.readers[k] = {}
        for k in reads:
            rk = eng if fill is None else ("dma", id(rec))
            self.readers.setdefault(k, {})[rk] = rec
        out = []
        for d in deps.values():
            if d is rec:
                continue
            if d.fill is None and d.eng == "pe" and eng == "pe" and fill is None:
                continue
            d.sig = True
            out.append((d, d.fill.count if d.fill is not None else None))
        rec.deps = out
        if fill is not None:
            fill.count += 16
            rec.ftarget = fill.count
        self.streams[eng].append(rec)
        return rec

    def finalize(self):
        for e in self.ENG:
            c = 0
            for r in self.streams[e]:
                if r.fill is None and r.sig:
                    c += 1
                    r.cnt = c

    def emit_engine(self, e, eng):
        waited = {}
        for r in self.streams[e]:
            ws = {}
            for d, snap in r.deps:
                if d.fill is not None:
                    nm, sem, val = d.fill.name, d.fill.sem, snap
                else:
                    nm, sem, val = d.eng, self.esem[d.eng], d.cnt
                if ws.get(nm, (None, 0))[1] < val:
                    ws[nm] = (sem, val)
            for nm, (sem, val) in ws.items():
                if waited.get(nm, 0) >= val:
                    continue
                waited[nm] = val
                eng.wait_ge(sem, val)
            if r.emit is not None:
                inst = r.emit(eng)
                if r.fill is not None:
                    inst.then_inc(r.fill.sem, 16)
                elif r.sig:
                    inst.then_inc(self.esem[e], 1)


def _rot(w, nheads):
    w4 = w.reshape(w.shape[0], nheads, 2, 32)
    return np.ascontiguousarray(w4[:, :, ::-1, :]).reshape(w.shape[0], nheads * 64)


def _tile_k(wc):
    n = wc.shape[1]
    return np.ascontiguousarray(wc.reshape(KC, 128, n).transpose(1, 0, 2)).reshape(128, KC * n)


def _ffn_tiles(w_gu, w_down):
    tiles = []
    for g0 in range(0, NFC, GF):
        fcs = list(range(g0, min(g0 + GF, NFC)))
        for i in range(0, len(fcs), 2):
            pr = fcs[i:i + 2]
            tiles.append(np.concatenate(
                [_tile_k(np.concatenate([w_gu[:, 128 * f:128 * f + 128],
                                         w_gu[:, DFF + 128 * f:DFF + 128 * f + 128]], axis=1))
                 for f in pr], axis=1))
        tiles.append(np.concatenate([w_down[128 * f:128 * f + 128, :] for f in fcs], axis=1))
    return tiles


def _nsa_stream(w_in, w1, w_o, w_gu, w_down):
    q = w_in[:, 0:1024]
    kc = w_in[:, 1024:1280]
    vc = w_in[:, 1280:1536]
    ks = w_in[:, 1536:1792]
    vs = w_in[:, 1792:2048]
    kw = w_in[:, 2048:2304]
    vw = w_in[:, 2304:2560]
    gt = w_in[:, 2560:2608]
    tiles = []
    for g in range(4):
        sl = slice(64 * g, 64 * g + 64)
        tiles.append(_tile_k(np.concatenate([kc[:, sl], vc[:, sl], _rot(kc[:, sl], 1)], axis=1)))
    w1t = np.ascontiguousarray(w1.reshape(2, 32, 64, 256).transpose(0, 2, 1, 3)).reshape(128, 32 * 256)
    tiles.append(w1t[:, 0:4096])
    tiles.append(w1t[:, 4096:8192])
    for g in range(4):
        sl = slice(64 * g, 64 * g + 64)
        ksg, kwg = ks[:, sl], kw[:, sl]
        tiles.append(_tile_k(np.concatenate(
            [ksg, ksg, _rot(ksg, 1), _rot(ksg, 1), kwg, kwg, _rot(kwg, 1), _rot(kwg, 1)], axis=1)))
        tiles.append(_tile_k(np.concatenate([vs[:, sl], vw[:, sl], gt[:, 12 * g:12 * g + 12]], axis=1)))
        qa = q[:, 256 * g:256 * g + 128]
        qb = q[:, 256 * g + 128:256 * g + 256]
        tiles.append(_tile_k(np.concatenate([qa, _rot(qa, 2), qb, _rot(qb, 2)], axis=1)))
        tiles.append(np.concatenate([w_o[256 * g:256 * g + 128, :], w_o[256 * g + 128:256 * g + 256, :]], axis=1))
    tiles += _ffn_tiles(w_gu, w_down)
    return tiles


def _swa_stream(w_qkv, w_o, w_gu, w_down):
    q = w_qkv[:, 0:1024]
    k = w_qkv[:, 1024:1152]
    v = w_qkv[:, 1152:1280]
    tiles = [_tile_k(v)]
    for g in range(2):
        kg = k[:, 64 * g:64 * g + 64]
        tiles.append(_tile_k(np.concatenate([kg, kg, _rot(kg, 1), _rot(kg, 1)], axis=1)))
        for pp in range(2):
            c0 = 4 * g + 2 * pp
            qa = q[:, 128 * c0:128 * c0 + 128]
            qb = q[:, 128 * c0 + 128:128 * c0 + 256]
            tiles.append(_tile_k(np.concatenate([qa, _rot(qa, 2), qb, _rot(qb, 2)], axis=1)))
            tiles.append(np.concatenate([w_o[128 * c0:128 * c0 + 128, :], w_o[128 * c0 + 128:128 * c0 + 256, :]], axis=1))
    tiles += _ffn_tiles(w_gu, w_down)
    return tiles


class PRM:
    GMIX = 0
    GFFN = 32
    GFIN = 64
    CW = 72
    CB = CW + 264
    GB = CB + 88
    BQ = GB + 96
    BQR = BQ + 16
    BK = BQR + 16
    BKR = BK + 4
    BO = BKR + 4
    BV = BO + 16
    SK = BV + 256
    EPSC = SK + 32
    ZERO = EPSC + 1
    TINY = ZERO + 1
    N = TINY + 1


class CBF:
    ID = 0
    MC = 128
    ME = 256
    MCMP = 384
    OVL = MCMP + 2048
    E = OVL + 40
    ONESB = E + 2048
    N = ONESB + 128


class CF:
    ONES = 0
    CM = 128
    ADDC = 128 + 256
    N = 128 + 512


def _consts():
    pos = np.arange(S, dtype=np.float32)
    inv = (10000.0 ** (-np.arange(0, 64, 2, dtype=np.float32) / 64.0)).astype(np.float32)
    ang = pos[:, None] * inv[None, :]
    cos = np.cos(ang).astype(np.float32).T
    sin = np.sin(ang).astype(np.float32).T
    cosT = np.concatenate([cos, cos, cos, cos], axis=0)
    sinT = np.concatenate([-sin, sin, -sin, sin], axis=0)
    cbf = np.zeros((128, CBF.N), np.float32)
    kk = np.arange(128)[:, None]
    qq = np.arange(128)[None, :]
    cbf[:, CBF.ID:CBF.ID + 128] = np.eye(128, dtype=np.float32)
    cbf[:, CBF.MC:CBF.MC + 128] = np.where(kk <= qq, 1.0, 0.0)
    cbf[:, CBF.ME:CBF.ME + 128] = np.where(kk > qq, 1.0, 0.0)
    c = np.arange(128)[:, None]
    t = np.arange(S)[None, :]
    cbf[:, CBF.MCMP:CBF.MCMP + S] = np.where(16 * c + 31 <= t, 0.0, -BIG)
    n = np.arange(32)[None, :]
    ovl = ((16 * c <= 64 * n + 63) & (16 * c + 31 >= 64 * n)).astype(np.float32)
    ovl[127, :] = 0.0
    cbf[:, CBF.OVL:CBF.OVL + 32] = ovl
    cbf[:, CBF.OVL + 32] = 1.0
    E = np.zeros((128, 16, 128), np.float32)
    for j in range(16):
        for k2 in range(128):
            E[2 * j + k2 // 64, j, k2] = BIG
    cbf[:, CBF.E:CBF.E + 2048] = E.reshape(128, 2048)
    cbf[:, CBF.ONESB:CBF.ONESB + 128] = 1.0
    cf = np.zeros((128, CF.N), np.float32)
    cf[:, CF.ONES:CF.ONES + 128] = 1.0
    cm = np.zeros((128, 8, 32), np.float32)
    addc = np.zeros((128, 8, 32), np.float32)
    for i in range(8, 16):
        for p in range(128):
            cur = (128 * i + p) // 64
            for nb in range(32):
                if nb <= cur:
                    cm[p, i - 8, nb] = 1.0
                    if nb == 0 or nb == cur or nb == cur - 1:
                        addc[p, i - 8, nb] = 1e4
                else:
                    addc[p, i - 8, nb] = -1.0
    cf[:, CF.CM:CF.CM + 256] = cm.reshape(128, 256)
    cf[:, CF.ADDC:CF.ADDC + 256] = addc.reshape(128, 256)
    return cosT, sinT, cbf, cf


def _pp(v, n):
    return np.ascontiguousarray(v.reshape(n, 128).T)


def _params(inp):
    prm = np.zeros((128, PRM.N), np.float32)
    for l in range(4):
        prm[:, PRM.GMIX + 8 * l:PRM.GMIX + 8 * l + 8] = _pp(inp["norm_mix"][l], 8)
        prm[:, PRM.GFFN + 8 * l:PRM.GFFN + 8 * l + 8] = _pp(inp["norm_ffn"][l], 8)
        cw = inp["ffn_conv_w"][l]
        prm[:, PRM.CW + 66 * l:PRM.CW + 66 * l + 66] = np.ascontiguousarray(
            cw.reshape(3, NFC, 128).transpose(2, 1, 0)).reshape(128, 66)
        prm[:, PRM.CB + 22 * l:PRM.CB + 22 * l + 22] = _pp(inp["ffn_conv_b"][l], NFC)
    prm[:, PRM.GFIN:PRM.GFIN + 8] = _pp(inp["norm_final"], 8)
    for j in range(2):
        prm[:, PRM.GB + 48 * j:PRM.GB + 48 * j + 48] = np.broadcast_to(inp["nsa_gate_b"][j][None, :], (128, 48))
        b = inp["swa_b_qkv"][j]
        bq = b[0:1024]
        bk = b[1024:1152]
        bv = b[1152:1280]
        prm[:, PRM.BQ + 8 * j:PRM.BQ + 8 * j + 8] = _pp(bq, 8)
        prm[:, PRM.BQR + 8 * j:PRM.BQR + 8 * j + 8] = _pp(_rot(bq[None, :], 16)[0], 8)
        bkr = _rot(bk[None, :], 2)[0]
        for g in range(2):
            prm[:, PRM.BK + 2 * j + g] = np.concatenate([bk[64 * g:64 * g + 64]] * 2)
            prm[:, PRM.BKR + 2 * j + g] = np.concatenate([bkr[64 * g:64 * g + 64]] * 2)
        prm[:, PRM.BO + 8 * j:PRM.BO + 8 * j + 8] = _pp(inp["swa_b_o"][j], 8)
        prm[:, PRM.BV + 128 * j:PRM.BV + 128 * j + 128] = np.broadcast_to(bv[None, :], (128, 128))
        prm[:, PRM.SK + 16 * j:PRM.SK + 16 * j + 16] = np.broadcast_to(inp["swa_sinks"][j][None, :], (128, 16))
    prm[:, PRM.EPSC] = EPS
    prm[:, PRM.ZERO] = 0.0
    prm[:, PRM.TINY] = 1e-30
    return prm


def _nsa_small(w2, pe):
    sm = np.zeros((128, 416), np.float32)
    w2k = w2[0].reshape(2, 128, 64)
    w2v = w2[1].reshape(2, 128, 64)
    for hc in range(2):
        sm[:, hc * 128:hc * 128 + 64] = w2k[hc]
        sm[:, hc * 128 + 64:hc * 128 + 128] = w2k[hc]
        sm[:, 256 + hc * 64:256 + hc * 64 + 64] = w2v[hc]
    sm[:, 384:416] = np.ascontiguousarray(pe.transpose(0, 2, 1)).reshape(128, 32)
    return sm


def build_program(wcols, nseq=2, nlayers=DEPTH, final_norm=True):
    nc = bass.Bass("TRN2", target_bir_lowering=False)
    x_d = nc.dram_tensor("xT", [2, KC, 128, S], F32, kind="ExternalInput").ap()
    y_d = nc.dram_tensor("yT", [2, KC, 128, S], F32, kind="ExternalOutput").ap()
    w_d = [nc.dram_tensor("w%d" % l, [128, wcols[l]], F32, kind="ExternalInput").ap() for l in range(DEPTH)]
    sm_d = nc.dram_tensor("nsasm", [2, 128, 416], F32, kind="ExternalInput").ap()
    prm_d = nc.dram_tensor("prm", [128, PRM.N], F32, kind="ExternalInput").ap()
    cos_d = nc.dram_tensor("cosT", [128, S], F32, kind="ExternalInput").ap()
    sin_d = nc.dram_tensor("sinT", [128, S], F32, kind="ExternalInput").ap()
    cbf_d = nc.dram_tensor("cbf", [128, CBF.N], F32, kind="ExternalInput").ap()
    cf_d = nc.dram_tensor("cf", [128, CF.N], F32, kind="ExternalInput").ap()

    from contextlib import ExitStack
    with ExitStack() as es:
        def sb(name, shape, dt):
            return es.enter_context(nc.sbuf_tensor(name, shape, dt))

        xT = sb("xT_sb", [128, KC, S], F32)
        hT = sb("hT_sb", [128, KC, S], BF16)
        cosT = sb("cos_sb", [128, S], F32)
        sinT = sb("sin_sb", [128, S], F32)
        prm = sb("prm_sb", [128, PRM.N], F32)
        cf = sb("cf_sb", [128, CF.N], F32)
        cbf = sb("cbf_sb", [128, CBF.N], BF16)
        wring = sb("wring", [128, NSLOT, SLOTC], BF16)
        nsm = sb("nsm", [128, 416], BF16)
        RA = sb("RA", [128, 16, 512], BF16)
        tf = sb("tf", [128, 6, 516], F32)
        ZQ = sb("ZQ", [128, 2, 2, 512], BF16)
        kcmpT = sb("kcmpT", [128, 4, 128], BF16)
        vcmp = sb("vcmp", [128, 4, 65], BF16)
        hidT = sb("hidT", [128, 4, 128], BF16)
        gtmp = sb("gtmp", [128, 4, 128], F32)
        peb = sb("peb", [128, 4], F32)
        Vt = sb("Vt", [128, NTT, 2, 65], BF16)
        gates = sb("gates", [128, NTT, 12], F32)
        Pcmp = sb("Pcmp", [128, 4, 512], BF16)
        PT = sb("PT", [128, NPT, 512], BF16)
        selT = sb("selT", [128, 512], BF16)
        st = sb("st", [128, 160], F32)
        sc = sb("sc", [128, 3, 32], F32)
        selm = sb("selm", [128, 32], BF16)
        opair = sb("opair", [128, 2, 128], BF16)
        otmp = sb("otmp", [128, 2, 64], F32)
        esk = sb("esk", [128, 16], F32)
        ps = es.enter_context(nc.psum_tensor("ps", [128, 8, 512], F32))
        psb = ps.bitcast(BF16)

        esem = {e: es.enter_context(nc.semaphore("s_" + e)) for e in ("pe", "act", "dve", "pool")}
        P = Prog(nc, esem)

        def ctr(name):
            return Ctr(es.enter_context(nc.semaphore(name)), name)

        c_const = ctr("c_const")
        c_x = ctr("c_x")
        c_out = [ctr("c_out0"), ctr("c_out1")]
        c_slot = [ctr("c_slot%d" % i) for i in range(NSLOT)]
        c_sm = ctr("c_sm")
        c_nsm = ctr("c_nsm")

        def mm(out, lhsT, rhs, start, stop, reads, writes):
            P.op("pe", lambda e: e.matmul(out, lhsT, rhs, start=start, stop=stop,
                                          skip_group_check=True), reads, writes)

        def tr(out, in_, ident, reads, writes):
            P.op("pe", lambda e: e.transpose(out, in_, ident), reads, writes)

        def act(out, in_, func, reads, writes, bias=None, scale=None):
            kw = {}
            if bias is not None:
                kw["bias"] = bias
            if scale is not None:
                kw["scale"] = scale
            P.op("act", lambda e: e.activation(out, in_, func, **kw), reads, writes)

        def ts(out, in0, s1, s2, op0, op1, reads, writes, eng="dve"):
            if op1 is None:
                P.op(eng, lambda e: e.tensor_scalar(out, in0, s1, None, op0), reads, writes)
            else:
                P.op(eng, lambda e: e.tensor_scalar(out, in0, s1, s2, op0, op1), reads, writes)

        def tt(out, in0, in1, op, reads, writes, eng="dve"):
            P.op(eng, lambda e: e.tensor_tensor(out, in0, in1, op), reads, writes)

        def stt(out, in0, scalar, in1, op0, op1, reads, writes, eng="dve"):
            P.op(eng, lambda e: e.scalar_tensor_tensor(out, in0, scalar, in1, op0, op1), reads, writes)

        def dma(q, out, in_, reads, writes, fill):
            P.op(q, lambda e: e.dma_start(out=out, in_=in_), reads, writes, fill=fill)

        bank_state = {"i": 0, "lo": 0}

        def nbank(hi=8):
            lo = bank_state["lo"]
            b = lo + bank_state["i"] % (hi - lo)
            bank_state["i"] += 1
            return b

        def pk(b):
            return ("ps", b)

        wstate = {"n": 0, "off": 0, "layer": 0}

        def wtile(ncols):
            slot = wstate["n"] % NSLOT
            wstate["n"] += 1
            l = wstate["layer"]
            off = wstate["off"]
            wstate["off"] += ncols
            dma("pool", wring[:, slot, 0:ncols], w_d[l][:, off:off + ncols], [], [("w", slot)], c_slot[slot])
            return wring[:, slot, :], ("w", slot)

        dma("sp", prm[:, :], prm_d[:, :], [], ["prm"], c_const)
        dma("sp", cf[:, :], cf_d[:, :], [], ["cf"], c_const)
        dma("sp", cosT[:, :], cos_d[:, :], [], ["cos"], c_const)
        dma("sp", sinT[:, :], sin_d[:, :], [], ["sin"], c_const)
        dma("pool", cbf[:, :], cbf_d[:, :], [], ["cbf"], c_sm)
        P.op("dve", lambda e: e.memset(selT[:, :], 0.0), [], ["selT"])
        P.op("dve", lambda e: e.memset(ZQ[:, :, :, :], 0.0), [], [("ZQ", 0), ("ZQ", 1)])

        def fill_zq(qT, Q):
            for pc in range(2):
                src = qT[:, pc * S + 512 * Q:pc * S + 512 * Q + 512]
                P.op("pool", lambda e, o=ZQ[0:64, pc, 0, :], i=src[0:64]: e.tensor_copy(o, i),
                     [("RA", pc * 4 + Q)], [("ZQ", pc)])
                P.op("pool", lambda e, o=ZQ[64:128, pc, 1, :], i=src[64:128]: e.tensor_copy(o, i),
                     [("RA", pc * 4 + Q)], [("ZQ", pc)])
        P.op("dve", lambda e: e.memset(vcmp[:, :, 64:65], 1.0), [], ["vcmp"])
        P.op("dve", lambda e: e.memset(Vt[:, :, :, 64:65], 1.0), [], ["Vt"])

        ident = cbf[:, CBF.ID:CBF.ID + 128]
        Mc = cbf[:, CBF.MC:CBF.MC + 128]
        Me = cbf[:, CBF.ME:CBF.ME + 128]
        ones32 = cf[:, CF.ONES:CF.ONES + 128]
        onesb = cbf[:, CBF.ONESB:CBF.ONESB + 128]

        def xk(c, tb):
            return ("xT", c, tb)

        def hk(c, tb):
            return ("hT", c, tb)

        def tfk(i):
            return ("tf", i)

        def cols(tb):
            return slice(512 * tb, 512 * tb + 512)

        def norm(gcol, out_fn):
            for tb in range(NTB):
                b = nbank()
                for c in range(KC):
                    sq = tf[:, c % 2, :].bitcast(BF16)[:, 0:512]
                    act(sq, xT[:, c, cols(tb)], AF.Square, [xk(c, tb)], [tfk(c % 2)])
                    mm(ps[:, b, :], onesb, sq, c == 0, c == KC - 1, [tfk(c % 2), "cbf"], [pk(b)])
                rt = tf[:, 2, 0:512]
                act(rt, ps[:, b, :], AF.Sqrt, ["prm"], [pk(b), tfk(2)],
                    bias=prm[:, PRM.EPSC:PRM.EPSC + 1], scale=1.0 / D)
                rstd = tf[:, 3, 0:512]
                P.op("dve", lambda e, o=rstd, i=rt: e.reciprocal(o, i), [tfk(2)], [tfk(3)])
                for c in range(KC):
                    out_fn(c, tb, rstd, prm[:, gcol + c:gcol + c + 1])

        def norm_to_hT(gcol):
            def f(c, tb, rstd, g):
                stt(hT[:, c, cols(tb)], xT[:, c, cols(tb)], g, rstd, ALU.mult, ALU.mult,
                    [xk(c, tb), tfk(3), "prm"], [hk(c, tb)])
            norm(gcol, f)

        def proj(b, M, wt, wkey, coloff, ncol_tile, tb):
            for kc in range(KC):
                mm(ps[0:M, b, :], wt[:, kc * ncol_tile + coloff:kc * ncol_tile + coloff + M],
                   hT[:, kc, cols(tb)], kc == 0, kc == KC - 1, [wkey, hk(kc, tb)], [pk(b)])

        def rope_evac(out, okey, b1, b2, rows, tb, bias1, bias2):
            r0, r1 = rows
            t1 = tf[r0:r1, 4, 0:512]
            t2 = tf[r0:r1, 5, 0:512]
            stt(t1, ps[r0:r1, b1, :], bias1[r0:r1], cosT[r0:r1, cols(tb)], ALU.add, ALU.mult,
                ["cos", "prm"], [pk(b1), tfk(4)])
            stt(t2, ps[r0:r1, b2, :], bias2[r0:r1], sinT[r0:r1, cols(tb)], ALU.add, ALU.mult,
                ["sin", "prm"], [pk(b2), tfk(5)])
            tt(out, t1, t2, ALU.add, [tfk(4), tfk(5)], list(okey))

        zero_b = prm[:, PRM.ZERO:PRM.ZERO + 1]

        from collections import deque
        pend = deque()
        LA = 4
        ptc = {"i": 0}

        def push(fn):
            pend.append(fn)
            while len(pend) > LA:
                pend.popleft()()

        def flush():
            while pend:
                pend.popleft()()

        def attn_pair_Q(Q, pcz, branches, pre_pv=None):
            nbr = len(branches) + (1 if pre_pv is not None else 0)
            first = [True] * 4

            def acc_mm(i, r, lhsT, rhs, reads):
                ti = i - 4 * Q
                mm(ps[:, ti, r * 65:r * 65 + 65], lhsT, rhs, first[ti], True, reads, [pk(ti)])
                first[ti] = False

            if pre_pv is not None:
                def pre():
                    for hh in range(2):
                        for i in range(4 * Q, 4 * Q + 4):
                            pre_pv(hh, i, lambda lhsT, rhs, reads, i=i, hh=hh: acc_mm(i, hh * nbr, lhsT, rhs, reads))
                push(pre)
            br0 = 1 if pre_pv is not None else 0
            for bi, br in enumerate(branches):
                W = br["win"]
                jlo = 0 if W is None else max(0, 4 * Q - W)
                for j in range(jlo, 4 * Q + 4):
                    ilo = max(j, 4 * Q)
                    ihi = 4 * Q + 3 if W is None else min(j + W, 4 * Q + 3)
                    if ihi < ilo:
                        continue
                    c0 = 128 * (ilo - 4 * Q)
                    c1 = 128 * (ihi - 4 * Q + 1)
                    for hh in range(2):
                        rows = slice(64 * hh, 64 * hh + 64)
                        b = nbank()
                        mm(ps[:, b, c0:c1], br["kT"][:, 128 * j:128 * j + 128],
                           ZQ[:, pcz, hh, c0:c1], True, True,
                           [br["kkey"](j), ("ZQ", pcz)], [pk(b)])
                        if br["sel"] and Q >= 2:
                            mm(ps[:, b, c0:c1], cbf[:, CBF.E + 128 * j:CBF.E + 128 * j + 128],
                               selT[:, c0:c1], False, True, ["cbf", "selT"], [pk(b)])
                        pi = ptc["i"] % NPT
                        ptc["i"] += 1
                        pt = PT[:, pi, :]
                        act(pt[:, c0:c1], ps[:, b, c0:c1], AF.Exp, [], [pk(b), ("PT", pi)], scale=SCALE)
                        if j >= 4 * Q:
                            cc = 128 * (j - 4 * Q)
                            tt(pt[:, cc:cc + 128], pt[:, cc:cc + 128], Mc, ALU.mult, ["cbf"], [("PT", pi)],
                               eng=("pool" if pi % 2 == 0 else "dve"))
                        if W is not None and 4 * Q <= j + W <= 4 * Q + 3:
                            cc = 128 * (j + W - 4 * Q)
                            tt(pt[:, cc:cc + 128], pt[:, cc:cc + 128], Me, ALU.mult, ["cbf"], [("PT", pi)],
                               eng=("pool" if pi % 2 == 0 else "dve"))

                        def pv(pt=pt, pi=pi, ilo=ilo, ihi=ihi, j=j, hh=hh, bi=bi, br=br):
                            for i in range(ilo, ihi + 1):
                                cc = 128 * (i - 4 * Q)
                                acc_mm(i, hh * nbr + br0 + bi, pt[:, cc:cc + 128], Vt[:, j, br["v"], :],
                                       [("PT", pi), "Vt"])
                        push(pv)

        def out_proj_partial(bias_col):
            wt, wkey = wtile(2048)
            for dm in range(KC):
                for tb in range(NTB):
                    b = nbank()
                    for pc in range(2):
                        mm(ps[:, b, :], wt[:, pc * 1024 + dm * 128:pc * 1024 + dm * 128 + 128],
                           RA[:, pc * 4 + tb, :], pc == 0, pc == 1, [wkey, ("RA", pc * 4 + tb)], [pk(b)])
                    if bias_col is None:
                        tt(xT[:, dm, cols(tb)], xT[:, dm, cols(tb)], ps[:, b, :], ALU.add,
                           [xk(dm, tb)], [pk(b), xk(dm, tb)])
                    else:
                        stt(xT[:, dm, cols(tb)], ps[:, b, :], prm[:, bias_col + dm:bias_col + dm + 1],
                            xT[:, dm, cols(tb)], ALU.add, ALU.add, [xk(dm, tb), "prm"], [pk(b), xk(dm, tb)])

        def evac_pair(Q, c, pcl, nbr, gate_fn, sink_cols):
            for ti in range(4):
                i = 4 * Q + ti
                nr = 2 * nbr
                den = st[:, 0:nr]
                acc = ps[:, ti, 0:nr * 65]
                if sink_cols is None:
                    ts(den, ps[:, ti, 64:nr * 65:65], 1e-30, None, ALU.max, None, [], [pk(ti), "st0"])
                else:
                    tt(den, ps[:, ti, 64:nr * 65:65], esk[:, sink_cols[0]:sink_cols[1]], ALU.add,
                       ["esk"], [pk(ti), "st0"])
                rc = st[:, 8:8 + nr]
                P.op("dve", lambda e, o=rc, a=den: e.reciprocal(o, a), ["st0"], ["st1"])
                if gate_fn is not None:
                    cf_ = st[:, 16:16 + nr]
                    tt(cf_, rc, gate_fn(i), ALU.mult, ["st1", "gates"], ["st2"])
                    ckey = "st2"
                else:
                    cf_ = rc
                    ckey = "st1"
                ob = (c + ti) % 2
                for hh in range(2):
                    dst = opair[:, ob, 64 * hh:64 * hh + 64]
                    if nbr == 1:
                        r = hh
                        ts(dst, ps[:, ti, r * 65:r * 65 + 64], cf_[:, r:r + 1], None, ALU.mult, None,
                           [ckey], [pk(ti), ("opair", ob)])
                    else:
                        tmp = otmp[:, hh, :]
                        r = hh * nbr
                        ts(tmp, ps[:, ti, r * 65:r * 65 + 64], cf_[:, r:r + 1], None, ALU.mult, None,
                           [ckey], [pk(ti), ("otmp", hh)])
                        for k in range(1, nbr):
                            r = hh * nbr + k
                            o_ = dst if k == nbr - 1 else tmp
                            wk = ("opair", ob) if k == nbr - 1 else ("otmp", hh)
                            stt(o_, ps[:, ti, r * 65:r * 65 + 64], cf_[:, r:r + 1], tmp, ALU.mult, ALU.add,
                                [ckey, ("otmp", hh)], [pk(ti), wk])
                b = nbank()
                tr(psb[:, b, 0:128], opair[:, ob, :], ident, [("opair", ob), "cbf"], [pk(b)])
                idx = pcl * 4 + Q
                act(RA[:, idx, 128 * ti:128 * ti + 128], psb[:, b, 0:128], AF.Copy, [],
                    [pk(b), ("RA", idx), ("RAv", idx)])

        def nsa_mixer(l, j):
            norm_to_hT(PRM.GMIX + 8 * l)
            if STAGE <= 0:
                return
            dma("pool", nsm[:, :], sm_d[j, :, :], [], ["nsm"], c_nsm)
            for g in range(4):
                wt, wkey = wtile(KC * 192)
                for tb in range(NTB):
                    b1 = nbank()
                    b2 = nbank()
                    proj(b1, 128, wt, wkey, 0, 192, tb)
                    proj(b2, 64, wt, wkey, 128, 192, tb)
                    idx = g * 4 + tb
                    rope_evac(RA[0:64, idx, :], [("RA", idx)], b1, b2, (0, 64), tb, zero_b, zero_b)
                    act(RA[64:128, idx, :], ps[64:128, b1, :], AF.Copy, [], [pk(b1), ("RAv", idx)])
            if STAGE <= 1:
                return
            w1a, w1ak = wtile(SLOTC)
            w1b, w1bk = wtile(SLOTC)

            def w1(rows, jj, hc):
                t = w1a if jj < 16 else w1b
                o = (jj % 16) * 256 + hc * 128
                return t[rows, o:o + 128], (w1ak if jj < 16 else w1bk)
            for kv in range(2):
                rows = slice(64 * kv, 64 * kv + 64)
                for hc in range(2):
                    b = nbank()
                    for jj in range(32):
                        lw, lk = w1(rows, jj, hc)
                        mm(ps[:, b, 0:1], lw, nsm[rows, 384 + jj:385 + jj], jj == 0, jj == 31,
                           [lk, "nsm"], [pk(b)])
                    act(peb[:, kv * 2 + hc:kv * 2 + hc + 1], ps[:, b, 0:1], AF.Copy, [], [pk(b), "peb"])
            kcv = RA[:, :, :].rearrange("p a b -> p (a b)")
            for g in range(4):
                for kv in range(2):
                    rows = slice(64 * kv, 64 * kv + 64)
                    rkeys = [("RA" if kv == 0 else "RAv", g * 4 + tb) for tb in range(NTB)]
                    for hc in range(2):
                        b = nbank()
                        for jj in range(32):
                            lw, lk = w1(rows, jj, hc)
                            mm(ps[:, b, 0:127], lw, kcv[rows, g * S + jj:g * S + jj + 16 * 126 + 1:16],
                               jj == 0, jj == 31, [lk] + rkeys, [pk(b)])
                        hi = kv * 2 + hc
                        z = gtmp[:, 0, 0:127]
                        act(z, ps[:, b, 0:127], AF.Identity, ["peb"], [pk(b), "gt0"], bias=peb[:, hi:hi + 1])
                        z2 = gtmp[:, 1, 0:127]
                        tt(z2, z, z, ALU.mult, ["gt0"], ["gt1"])
                        ts(z2, z2, 0.044715, 1.0, ALU.mult, ALU.add, ["gt1"], ["gt1"])
                        tt(z2, z2, z, ALU.mult, ["gt0", "gt1"], ["gt1"])
                        sg = gtmp[:, 2, 0:127]
                        act(sg, z2, AF.Sigmoid, ["gt1"], ["gt2"], scale=1.5957691216057308)
                        tt(hidT[:, hi, 0:127], sg, z, ALU.mult, ["gt0", "gt2"], [("hid", hi)])
                    b = nbank()
                    if kv == 0:
                        for hc in range(2):
                            mm(ps[:, b, 0:127], nsm[:, hc * 128:hc * 128 + 128], hidT[:, hc, 0:127],
                               hc == 0, hc == 1, ["nsm", ("hid", hc)], [pk(b)])
                        act(kcmpT[:, g, 0:127], ps[:, b, 0:127], AF.Copy, [], [pk(b), ("kcmp", g)])
                    else:
                        for hc in range(2):
                            mm(ps[0:127, b, 0:64], hidT[:, 2 + hc, 0:127], nsm[:, 256 + hc * 64:256 + hc * 64 + 64],
                               hc == 0, hc == 1, ["nsm", ("hid", 2 + hc)], [pk(b)])
                        act(vcmp[0:127, g, 0:64], ps[0:127, b, 0:64], AF.Copy, ["vcmp"], [pk(b), ("vcmpg", g)])
            if STAGE <= 2:
                return
            qT = RA[:, 0:8, :].rearrange("p a b -> p (a b)")
            ksT = RA[:, 8:12, :].rearrange("p a b -> p (a b)")
            kwT = RA[:, 12:16, :].rearrange("p a b -> p (a b)")
            for g in range(4):
                wt, wkey = wtile(SLOTC)
                for tb in range(NTB):
                    for which in range(2):
                        b1 = nbank()
                        b2 = nbank()
                        proj(b1, 128, wt, wkey, which * 256, 512, tb)
                        proj(b2, 128, wt, wkey, which * 256 + 128, 512, tb)
                        idx = 8 + which * 4 + tb
                        rope_evac(RA[:, idx, :], [("RA", idx), ("RAv", idx)], b1, b2, (0, 128), tb, zero_b, zero_b)
                wt, wkey = wtile(KC * 140)
                for i in range(NTT):
                    b = nbank()
                    for kc in range(KC):
                        mm(ps[:, b, 0:140], hT[:, kc, 128 * i:128 * i + 128], wt[:, kc * 140:kc * 140 + 140],
                           kc == 0, kc == KC - 1, [wkey, hk(kc, i // 4)], [pk(b)])
                    act(Vt[:, i, :, 0:64], ps[:, b, 0:128].rearrange("p (a d) -> p a d", a=2), AF.Copy,
                        [], [pk(b), "Vt"])
                    gs = st[:, 32:44]
                    tt(gs, ps[:, b, 128:140], prm[:, PRM.GB + 48 * j + 12 * g:PRM.GB + 48 * j + 12 * g + 12],
                       ALU.add, ["prm"], [pk(b), "st3"])
                    act(gates[:, i, :], gs, AF.Sigmoid, ["st3"], ["gates"])
                wt, wkey = wtile(SLOTC)
                for pc in range(2):
                    for tb in range(NTB):
                        b1 = nbank()
                        b2 = nbank()
                        proj(b1, 128, wt, wkey, pc * 256, 512, tb)
                        proj(b2, 128, wt, wkey, pc * 256 + 128, 512, tb)
                        idx = pc * 4 + tb
                        rope_evac(RA[:, idx, :], [("RA", idx), ("RAv", idx)], b1, b2, (0, 128), tb, zero_b, zero_b)
                if STAGE <= 3:
                    return
                bank_state["lo"] = 4
                for Q in range(NTB if STAGE > 4 else 1):
                    fill_zq(qT, Q)
                    for h in range(4):
                        pc, hh = h // 2, h % 2
                        b = nbank()
                        mm(ps[0:127, b, :], kcmpT[:, g, 0:127], ZQ[:, pc, hh, :],
                           True, False, [("kcmp", g), ("ZQ", pc)], [pk(b)])
                        mm(ps[0:127, b, :], cbf[:, CBF.ID:CBF.ID + 127],
                           cbf[:, CBF.MCMP + 512 * Q:CBF.MCMP + 512 * Q + 512], False, True, ["cbf"], [pk(b)])
                        act(Pcmp[0:127, h, :], ps[0:127, b, :], AF.Exp, [], [pk(b), ("Pcmp", h)], scale=SCALE)
                    if Q >= 2:
                        for ti in range(4):
                            i = 4 * Q + ti
                            b = nbank()
                            for h in range(4):
                                mm(ps[:, b, 33 * h:33 * h + 33], Pcmp[0:127, h, 128 * ti:128 * ti + 128],
                                   cbf[0:127, CBF.OVL:CBF.OVL + 33], h == 0, True, [("Pcmp", h), "cbf"], [pk(b)])
                            rc4 = st[:, 48:52]
                            P.op("dve", lambda e, o=rc4, a=ps[:, b, 32:132:33]: e.reciprocal(o, a), [], [pk(b), "st4"])
                            imp = sc[:, 0, :]
                            ts(imp, ps[:, b, 0:32], rc4[:, 0:1], None, ALU.mult, None, ["st4"], [pk(b), "sc0"])
                            for h in range(1, 4):
                                stt(imp, ps[:, b, 33 * h:33 * h + 32], rc4[:, h:h + 1], imp, ALU.mult, ALU.add,
                                    ["st4", "sc0"], [pk(b), "sc0"])
                            tt(imp, imp, cf[:, CF.CM + 32 * (i - 8):CF.CM + 32 * (i - 8) + 32], ALU.mult,
                               ["cf", "sc0"], ["sc0"])
                            tt(imp, imp, cf[:, CF.ADDC + 32 * (i - 8):CF.ADDC + 32 * (i - 8) + 32], ALU.add,
                               ["cf", "sc0"], ["sc0"])
                            m8 = st[:, 56:64]
                            P.op("dve", lambda e, o=m8, a=imp: e.max(o, a), ["sc0"], ["st5"])
                            s2 = sc[:, 1, :]
                            P.op("dve", lambda e, o=s2, r_=m8, a=imp: e.match_replace(o, r_, a, -1e9),
                                 ["sc0", "st5"], ["sc1"])
                            m8b = st[:, 64:72]
                            P.op("dve", lambda e, o=m8b, a=s2: e.max(o, a), ["sc1"], ["st6"])
                            thr = st[:, 72:73]
                            P.op("dve", lambda e, o=thr, a=m8b: e.tensor_reduce(o, a, AX.X, ALU.min), ["st6"], ["st7"])
                            ts(selm[:, :], imp, thr, -1.0, ALU.is_ge, ALU.add, ["sc0", "st7"], ["selm"])
                            b2 = nbank()
                            tr(psb[0:32, b2, 0:128], selm[:, :], ident, ["selm", "cbf"], [pk(b2)])
                            act(selT[0:32, 128 * ti:128 * ti + 128], psb[0:32, b2, 0:128], AF.Copy, [], [pk(b2), "selT"])
                    for pc in range(2):
                        c = 2 * g + pc

                        def pre_pv(hh, i, acc, pc=pc):
                            ti = i - 4 * Q
                            h = pc * 2 + hh
                            acc(Pcmp[0:127, h, 128 * ti:128 * ti + 128], vcmp[0:127, g, :],
                                [("Pcmp", h), ("vcmpg", g), "vcmp"])
                        branches = [
                            dict(kT=ksT, kkey=lambda jj: ("RA", 8 + jj // 4), v=0, win=None, sel=True),
                            dict(kT=kwT, kkey=lambda jj: ("RA", 12 + jj // 4), v=1, win=4, sel=False),
                        ]
                        attn_pair_Q(Q, pc, branches, pre_pv)
                        push(lambda Q=Q, c=c, pc=pc: evac_pair(Q, c, pc, 3, lambda i, pc=pc: gates[:, i, 6 * pc:6 * pc + 6], None))
                flush()
                bank_state["lo"] = 0
                out_proj_partial(None)

        def swa_mixer(l, j):
            norm_to_hT(PRM.GMIX + 8 * l)
            act(esk[:, :], prm[:, PRM.SK + 16 * j:PRM.SK + 16 * j + 16], AF.Exp, ["prm"], ["esk"])
            qT = RA[:, 0:8, :].rearrange("p a b -> p (a b)")
            kT = RA[:, 8:12, :].rearrange("p a b -> p (a b)")
            wt, wkey = wtile(KC * 128)
            for i in range(NTT):
                b = nbank()
                for kc in range(KC):
                    mm(ps[:, b, 0:128], hT[:, kc, 128 * i:128 * i + 128], wt[:, kc * 128:kc * 128 + 128],
                       kc == 0, kc == KC - 1, [wkey, hk(kc, i // 4)], [pk(b)])
                tt(Vt[:, i, :, 0:64], ps[:, b, 0:128].rearrange("p (a d) -> p a d", a=2),
                   prm[:, PRM.BV + 128 * j:PRM.BV + 128 * j + 128].rearrange("p (a d) -> p a d", a=2),
                   ALU.add, ["prm"], [pk(b), "Vt"])
            for g in range(2):
                wt, wkey = wtile(KC * 256)
                for tb in range(NTB):
                    b1 = nbank()
                    b2 = nbank()
                    proj(b1, 128, wt, wkey, 0, 256, tb)
                    proj(b2, 128, wt, wkey, 128, 256, tb)
                    idx = 8 + tb
                    rope_evac(RA[:, idx, :], [("RA", idx), ("RAv", idx)], b1, b2, (0, 128), tb,
                              prm[:, PRM.BK + 2 * j + g:PRM.BK + 2 * j + g + 1],
                              prm[:, PRM.BKR + 2 * j + g:PRM.BKR + 2 * j + g + 1])
                for pp in range(2):
                    wt, wkey = wtile(SLOTC)
                    for pc in range(2):
                        c = 4 * g + 2 * pp + pc
                        for tb in range(NTB):
                            b1 = nbank()
                            b2 = nbank()
                            proj(b1, 128, wt, wkey, pc * 256, 512, tb)
                            proj(b2, 128, wt, wkey, pc * 256 + 128, 512, tb)
                            idx = pc * 4 + tb
                            rope_evac(RA[:, idx, :], [("RA", idx), ("RAv", idx)], b1, b2, (0, 128), tb,
                                      prm[:, PRM.BQ + 8 * j + c:PRM.BQ + 8 * j + c + 1],
                                      prm[:, PRM.BQR + 8 * j + c:PRM.BQR + 8 * j + c + 1])
                        bank_state["lo"] = 4
                    for Q in range(NTB):
                        fill_zq(qT, Q)
                        for pc in range(2):
                            c = 4 * g + 2 * pp + pc
                            branches = [dict(kT=kT, kkey=lambda jj: ("RA", 8 + jj // 4), v=g, win=1, sel=False)]
                            attn_pair_Q(Q, pc, branches, None)
                            push(lambda Q=Q, c=c, pc=pc: evac_pair(Q, c, pc, 1, None, (2 * c, 2 * c + 2)))
                    flush()
                    bank_state["lo"] = 0
                    out_proj_partial(PRM.BO + 8 * j if (g == 0 and pp == 0) else None)

        def ffn(l):
            norm_to_hT(PRM.GFFN + 8 * l)
            actT = RA
            for g0 in range(0, NFC, GF):
                fcs = list(range(g0, min(g0 + GF, NFC)))
                wgu = {}
                for ii in range(0, len(fcs), 2):
                    wt, wkey = wtile(2048 * len(fcs[ii:ii + 2]))
                    for k2, f in enumerate(fcs[ii:ii + 2]):
                        wgu[f] = (wt, wkey, k2 * 2048)
                    for k2, f in enumerate(fcs[ii:ii + 2]):
                        fl = f - g0
                        wtt, wk, wo = wgu[f]
                        cw = PRM.CW + 66 * l + 3 * f
                        cb = PRM.CB + 22 * l + f
                        for tb in range(NTB):
                            ba = nbank()
                            bu = nbank()
                            for kc in range(KC):
                                mm(ps[:, ba, :], wtt[:, wo + kc * 256:wo + kc * 256 + 128], hT[:, kc, cols(tb)],
                                   kc == 0, kc == KC - 1, [wk, hk(kc, tb)], [pk(ba)])
                            for kc in range(KC):
                                mm(ps[:, bu, :], wtt[:, wo + kc * 256 + 128:wo + kc * 256 + 256], hT[:, kc, cols(tb)],
                                   kc == 0, kc == KC - 1, [wk, hk(kc, tb)], [pk(bu)])
                            ai = 4 + tb % 2
                            a_sb = tf[:, ai, :]
                            if tb == 0:
                                P.op("dve", lambda e, o=a_sb[:, 0:2]: e.memset(o, 0.0), [], [tfk(ai)])
                            else:
                                act(a_sb[:, 0:2], tf[:, 4 + (tb - 1) % 2, 512:514], AF.Copy,
                                    [tfk(4 + (tb - 1) % 2)], [tfk(ai)])
                            act(a_sb[:, 2:514], ps[:, ba, :], AF.Copy, [], [pk(ba), tfk(ai)])
                            y = tf[:, 0, 0:512]
                            ts(y, a_sb[:, 2:514], prm[:, cw + 2:cw + 3], prm[:, cb:cb + 1], ALU.mult, ALU.add,
                               [tfk(ai), "prm"], [tfk(0)])
                            stt(y, a_sb[:, 1:513], prm[:, cw + 1:cw + 2], y, ALU.mult, ALU.add,
                                [tfk(ai), "prm", tfk(0)], [tfk(0)])
                            stt(y, a_sb[:, 0:512], prm[:, cw:cw + 1], y, ALU.mult, ALU.add,
                                [tfk(ai), "prm", tfk(0)], [tfk(0)])
                            sl = tf[:, 1, 0:512]
                            act(sl, y, AF.Silu, [tfk(0)], [tfk(1)])
                            idx = fl * 4 + tb
                            tt(actT[:, idx, :], sl, ps[:, bu, :], ALU.mult, [tfk(1)], [pk(bu), ("RA", idx), ("RAv", idx)])
                wt, wkey = wtile(1024 * len(fcs))
                for dm in range(KC):
                    for tb in range(NTB):
                        b = nbank()
                        for fl in range(len(fcs)):
                            mm(ps[:, b, :], wt[:, fl * 1024 + dm * 128:fl * 1024 + dm * 128 + 128],
                               actT[:, fl * 4 + tb, :], fl == 0, fl == len(fcs) - 1, [wkey, ("RA", fl * 4 + tb)], [pk(b)])
                        tt(xT[:, dm, cols(tb)], xT[:, dm, cols(tb)], ps[:, b, :], ALU.add,
                           [xk(dm, tb)], [pk(b), xk(dm, tb)])

        for sq_ in range(nseq):
            for c in range(KC):
                dma("sp", xT[:, c, :], x_d[sq_, c, :, :], [], [xk(c, tb) for tb in range(NTB)], c_x)
            for l in range(nlayers):
                wstate["layer"] = l
                wstate["off"] = 0
                if l % 2 == 0:
                    nsa_mixer(l, l // 2)
                else:
                    swa_mixer(l, l // 2)
                if STAGE >= 10:
                    ffn(l)
            if final_norm:
                def fo(c, tb, rstd, g, sq_=sq_):
                    oi = (c + tb) % 2
                    o = tf[:, 4 + oi, 0:512]
                    stt(o, xT[:, c, cols(tb)], g, rstd, ALU.mult, ALU.mult,
                        [xk(c, tb), tfk(3), "prm"], [tfk(4 + oi)])
                    dma("sp", y_d[sq_, c, :, cols(tb)], o, [tfk(4 + oi)], [("y", sq_, c, tb)], c_out[oi])
                norm(PRM.GFIN, fo)
            else:
                for c in range(KC):
                    dma("sp", y_d[sq_, c, :, :], xT[:, c, :], [xk(c, tb) for tb in range(NTB)],
                        [("y", sq_, c, tb) for tb in range(NTB)], c_out[c % 2])
        P.op("sp", None, [("y", s_, c, tb) for s_ in range(nseq) for c in range(KC) for tb in range(NTB)], [])

        P.finalize()
        with nc.Block() as block:
            @block.tensor
            def _(e):
                P.emit_engine("pe", e)

            @block.scalar
            def _(e):
                P.emit_engine("act", e)

            @block.vector
            def _(e):
                P.emit_engine("dve", e)

            @block.gpsimd
            def _(e):
                P.emit_engine("pool", e)

            @block.sync
            def _(e):
                P.emit_engine("sp", e)
    return nc


def _prep(inp):
    inp = {k: np.asarray(v, dtype=np.float32) for k, v in inp.items()}
    streams = []
    for l in range(DEPTH):
        j = l // 2
        if l % 2 == 0:
            tiles = _nsa_stream(inp["nsa_w_in"][j], inp["nsa_cmp_w1"][j], inp["nsa_w_o"][j],
                                inp["ffn_w_gu"][l], inp["ffn_w_down"][l])
        else:
            tiles = _swa_stream(inp["swa_w_qkv"][j], inp["swa_w_o"][j], inp["ffn_w_gu"][l], inp["ffn_w_down"][l])
        streams.append(np.ascontiguousarray(np.concatenate(tiles, axis=1)))
    cosT, sinT, cbf, cf = _consts()
    shared = {"prm": _params(inp), "cosT": cosT, "sinT": sinT, "cbf": cbf, "cf": cf,
              "nsasm": np.stack([_nsa_small(inp["nsa_cmp_w2"][j], inp["nsa_cmp_pe"][j]) for j in range(2)])}
    for l in range(DEPTH):
        shared["w%d" % l] = streams[l]
    return inp, shared


def kernel(**inputs):
    inp, shared = _prep(inputs)
    x = inp["x"]
    nc = build_program([shared["w%d" % l].shape[1] for l in range(DEPTH)])
    in_maps = []
    for core in range(8):
        xs = x[2 * core:2 * core + 2]
        xT = np.ascontiguousarray(xs.transpose(0, 2, 1)).reshape(2, KC, 128, S)
        m = dict(shared)
        m["xT"] = xT
        in_maps.append(m)
    res = run_bass_kernel_spmd(nc, in_maps, core_ids=list(range(8)))
    out = np.empty((16, S, D), np.float32)
    for core in range(8):
        yT = res.results[core]["yT"].reshape(2, D, S)
        out[2 * core:2 * core + 2] = yT.transpose(0, 2, 1)
    return out
```

```python
import numpy as np
import concourse.bass as bass
import concourse.mybir as mybir
from concourse.bass_utils import run_bass_kernel_spmd

F32 = mybir.dt.float32
BF16 = mybir.dt.bfloat16
F32R = mybir.dt.float32r
AF = mybir.ActivationFunctionType
ALU = mybir.AluOpType
AX = mybir.AxisListType

S = 2048
D = 1024
KC = 8
NTB = 4
NTT = 16
DFF = 2816
NFC = 22
DEPTH = 4
BIG = 32768.0
SCALE = 0.125
EPS = 1e-6
NSLOT = 3
SLOTC = 4096
NPT = 6
GF = 4
STAGE = 99


class Rec:
    __slots__ = ("eng", "emit", "deps", "sig", "cnt", "fill", "ftarget")


class Ctr:
    def __init__(self, sem, name):
        self.sem = sem
        self.name = name
        self.count = 0


class Prog:
    ENG = ("pe", "act", "dve", "pool", "sp")

    def __init__(self, nc, esem):
        self.nc = nc
        self.esem = esem
        self.streams = {e: [] for e in self.ENG}
        self.last_w = {}
        self.readers = {}

    def op(self, eng, emit, reads=(), writes=(), fill=None):
        rec = Rec()
        rec.eng = eng
        rec.emit = emit
        rec.sig = False
        rec.cnt = None
        rec.fill = fill
        rec.ftarget = None
        deps = {}
        for k in reads:
            w = self.last_w.get(k)
            if w is not None:
                deps[id(w)] = w
        for k in writes:
            w = self.last_w.get(k)
            if w is not None:
                deps[id(w)] = w
            rd = self.readers.get(k)
            if rd:
                for r in rd.values():
                    deps[id(r)] = r
        for k in writes:
            self.last_w[k] = rec
            self.readers[k] = {}
        for k in reads:
            rk = eng if fill is None else ("dma", id(rec))
            self.readers.setdefault(k, {})[rk] = rec
        out = []
        for d in deps.values():
            if d is rec:
                continue
            if d.fill is None and d.eng == "pe" and eng == "pe" and fill is None:
                continue
            d.sig = True
            out.append((d, d.fill.count if d.fill is not None else None))
        rec.deps = out
        if fill is not None:
            fill.count += 16
            rec.ftarget = fill.count
        self.streams[eng].append(rec)
        return rec

    def finalize(self):
        for e in self.ENG:
            c = 0
            for r in self.streams[e]:
                if r.fill is None and r.sig:
                    c += 1
                    r.cnt = c

    def emit_engine(self, e, eng):
        waited = {}
        for r in self.streams[e]:
            ws = {}
            for d, snap in r.deps:
                if d.fill is not None:
                    nm, sem, val = d.fill.name, d.fill.sem, snap
                else:
                    nm, sem, val = d.eng, self.esem[d.eng], d.cnt
                if ws.get(nm, (None, 0))[1] < val:
                    ws[nm] = (sem, val)
            for nm, (sem, val) in ws.items():
                if waited.get(nm, 0) >= val:
                    continue
                waited[nm] = val
                eng.wait_ge(sem, val)
            if r.emit is not None:
                inst = r.emit(eng)
                if r.fill is not None:
                    inst.then_inc(r.fill.sem, 16)
                elif r.sig:
                    inst.then_inc(self.esem[e], 1)


def _rot(w, nheads):
    w4 = w.reshape(w.shape[0], nheads, 2, 32)
    return np.ascontiguousarray(w4[:, :, ::-1, :]).reshape(w.shape[0], nheads * 64)


def _tile_k(wc):
    n = wc.shape[1]
    return np.ascontiguousarray(wc.reshape(KC, 128, n).transpose(1, 0, 2)).reshape(128, KC * n)


def _ffn_tiles(w_gu, w_down):
    tiles = []
    for g0 in range(0, NFC, GF):
        fcs = list(range(g0, min(g0 + GF, NFC)))
        for i in range(0, len(fcs), 2):
            pr = fcs[i:i + 2]
            tiles.append(np.concatenate(
                [_tile_k(np.concatenate([w_gu[:, 128 * f:128 * f + 128],
                                         w_gu[:, DFF + 128 * f:DFF + 128 * f + 128]], axis=1))
                 for f in pr], axis=1))
        tiles.append(np.concatenate([w_down[128 * f:128 * f + 128, :] for f in fcs], axis=1))
    return tiles


def _nsa_stream(w_in, w1, w_o, w_gu, w_down):
    q = w_in[:, 0:1024]
    kc = w_in[:, 1024:1280]
    vc = w_in[:, 1280:1536]
    ks = w_in[:, 1536:1792]
    vs = w_in[:, 1792:2048]
    kw = w_in[:, 2048:2304]
    vw = w_in[:, 2304:2560]
    gt = w_in[:, 2560:2608]
    tiles = []
    for g in range(4):
        sl = slice(64 * g, 64 * g + 64)
        tiles.append(_tile_k(np.concatenate([kc[:, sl], vc[:, sl], _rot(kc[:, sl], 1)], axis=1)))
    w1t = np.ascontiguousarray(w1.reshape(2, 32, 64, 256).transpose(0, 2, 1, 3)).reshape(128, 32 * 256)
    tiles.append(w1t[:, 0:4096])
    tiles.append(w1t[:, 4096:8192])
    for g in range(4):
        sl = slice(64 * g, 64 * g + 64)
        ksg, kwg = ks[:, sl], kw[:, sl]
        tiles.append(_tile_k(np.concatenate(
            [ksg, ksg, _rot(ksg, 1), _rot(ksg, 1), kwg, kwg, _rot(kwg, 1), _rot(kwg, 1)], axis=1)))
        tiles.append(_tile_k(np.concatenate([vs[:, sl], vw[:, sl], gt[:, 12 * g:12 * g + 12]], axis=1)))
        qa = q[:, 256 * g:256 * g + 128]
        qb = q[:, 256 * g + 128:256 * g + 256]
        tiles.append(_tile_k(np.concatenate([qa, _rot(qa, 2), qb, _rot(qb, 2)], axis=1)))
        tiles.append(np.concatenate([w_o[256 * g:256 * g + 128, :], w_o[256 * g + 128:256 * g + 256, :]], axis=1))
    tiles += _ffn_tiles(w_gu, w_down)
    return tiles


def _swa_stream(w_qkv, w_o, w_gu, w_down):
    q = w_qkv[:, 0:1024]
    k = w_qkv[:, 1024:1152]
    v = w_qkv[:, 1152:1280]
    tiles = [_tile_k(v)]
    for g in range(2):
        kg = k[:, 64 * g:64 * g + 64]
        tiles.append(_tile_k(np.concatenate([kg, kg, _rot(kg, 1), _rot(kg, 1)], axis=1)))
        for pp in range(2):
            c0 = 4 * g + 2 * pp
            qa = q[:, 128 * c0:128 * c0 + 128]
            qb = q[:, 128 * c0 + 128:128 * c0 + 256]
            tiles.append(_tile_k(np.concatenate([qa, _rot(qa, 2), qb, _rot(qb, 2)], axis=1)))
            tiles.append(np.concatenate([w_o[128 * c0:128 * c0 + 128, :], w_o[128 * c0 + 128:128 * c0 + 256, :]], axis=1))
    tiles += _ffn_tiles(w_gu, w_down)
    return tiles


class PRM:
    GMIX = 0
    GFFN = 32
    GFIN = 64
    CW = 72
    CB = CW + 264
    GB = CB + 88
    BQ = GB + 96
    BQR = BQ + 16
    BK = BQR + 16
    BKR = BK + 4
    BO = BKR + 4
    BV = BO + 16
    SK = BV + 256
    EPSC = SK + 32
    ZERO = EPSC + 1
    TINY = ZERO + 1
    N = TINY + 1


class CBF:
    ID = 0
    MC = 128
    ME = 256
    MCMP = 384
    OVL = MCMP + 2048
    E = OVL + 40
    ONESB = E + 2048
    MCN = ONESB + 128
    MEN = MCN + 128
    N = MEN + 128


class CF:
    ONES = 0
    CM = 128
    ADDC = 128 + 256
    N = 128 + 512


def _consts():
    pos = np.arange(S, dtype=np.float32)
    inv = (10000.0 ** (-np.arange(0, 64, 2, dtype=np.float32) / 64.0)).astype(np.float32)
    ang = pos[:, None] * inv[None, :]
    cos = np.cos(ang).astype(np.float32).T
    sin = np.sin(ang).astype(np.float32).T
    cosT = np.concatenate([cos, cos, cos, cos], axis=0)
    sinT = np.concatenate([-sin, sin, -sin, sin], axis=0)
    cbf = np.zeros((128, CBF.N), np.float32)
    kk = np.arange(128)[:, None]
    qq = np.arange(128)[None, :]
    cbf[:, CBF.ID:CBF.ID + 128] = np.eye(128, dtype=np.float32)
    cbf[:, CBF.MC:CBF.MC + 128] = np.where(kk <= qq, 1.0, 0.0)
    cbf[:, CBF.ME:CBF.ME + 128] = np.where(kk > qq, 1.0, 0.0)
    c = np.arange(128)[:, None]
    t = np.arange(S)[None, :]
    cbf[:, CBF.MCMP:CBF.MCMP + S] = np.where(16 * c + 31 <= t, 0.0, -BIG)
    n = np.arange(32)[None, :]
    ovl = ((16 * c <= 64 * n + 63) & (16 * c + 31 >= 64 * n)).astype(np.float32)
    ovl[127, :] = 0.0
    cbf[:, CBF.OVL:CBF.OVL + 32] = ovl
    cbf[:, CBF.OVL + 32] = 1.0
    E = np.zeros((128, 16, 128), np.float32)
    for j in range(16):
        for k2 in range(128):
            E[2 * j + k2 // 64, j, k2] = BIG
    cbf[:, CBF.E:CBF.E + 2048] = E.reshape(128, 2048)
    cbf[:, CBF.ONESB:CBF.ONESB + 128] = 1.0
    cbf[:, CBF.MCN:CBF.MCN + 128] = np.where(kk <= qq, 0.0, -BIG)
    cbf[:, CBF.MEN:CBF.MEN + 128] = np.where(kk > qq, 0.0, -BIG)
    cf = np.zeros((128, CF.N), np.float32)
    cf[:, CF.ONES:CF.ONES + 128] = 1.0
    cm = np.zeros((128, 8, 32), np.float32)
    addc = np.zeros((128, 8, 32), np.float32)
    for i in range(8, 16):
        for p in range(128):
            cur = (128 * i + p) // 64
            for nb in range(32):
                if nb <= cur:
                    cm[p, i - 8, nb] = 1.0
                    if nb == 0 or nb == cur or nb == cur - 1:
                        addc[p, i - 8, nb] = 1e4
                else:
                    addc[p, i - 8, nb] = -1.0
    cf[:, CF.CM:CF.CM + 256] = cm.reshape(128, 256)
    cf[:, CF.ADDC:CF.ADDC + 256] = addc.reshape(128, 256)
    return cosT, sinT, cbf, cf


def _pp(v, n):
    return np.ascontiguousarray(v.reshape(n, 128).T)


def _params(inp):
    prm = np.zeros((128, PRM.N), np.float32)
    for l in range(4):
        prm[:, PRM.GMIX + 8 * l:PRM.GMIX + 8 * l + 8] = _pp(inp["norm_mix"][l], 8)
        prm[:, PRM.GFFN + 8 * l:PRM.GFFN + 8 * l + 8] = _pp(inp["norm_ffn"][l], 8)
        cw = inp["ffn_conv_w"][l]
        prm[:, PRM.CW + 66 * l:PRM.CW + 66 * l + 66] = np.ascontiguousarray(
            cw.reshape(3, NFC, 128).transpose(2, 1, 0)).reshape(128, 66)
        prm[:, PRM.CB + 22 * l:PRM.CB + 22 * l + 22] = _pp(inp["ffn_conv_b"][l], NFC)
    prm[:, PRM.GFIN:PRM.GFIN + 8] = _pp(inp["norm_final"], 8)
    for j in range(2):
        prm[:, PRM.GB + 48 * j:PRM.GB + 48 * j + 48] = np.broadcast_to(inp["nsa_gate_b"][j][None, :], (128, 48))
        b = inp["swa_b_qkv"][j]
        bq = b[0:1024]
        bk = b[1024:1152]
        bv = b[1152:1280]
        prm[:, PRM.BQ + 8 * j:PRM.BQ + 8 * j + 8] = _pp(bq, 8)
        prm[:, PRM.BQR + 8 * j:PRM.BQR + 8 * j + 8] = _pp(_rot(bq[None, :], 16)[0], 8)
        bkr = _rot(bk[None, :], 2)[0]
        for g in range(2):
            prm[:, PRM.BK + 2 * j + g] = np.concatenate([bk[64 * g:64 * g + 64]] * 2)
            prm[:, PRM.BKR + 2 * j + g] = np.concatenate([bkr[64 * g:64 * g + 64]] * 2)
        prm[:, PRM.BO + 8 * j:PRM.BO + 8 * j + 8] = _pp(inp["swa_b_o"][j], 8)
        prm[:, PRM.BV + 128 * j:PRM.BV + 128 * j + 128] = np.broadcast_to(bv[None, :], (128, 128))
        prm[:, PRM.SK + 16 * j:PRM.SK + 16 * j + 16] = np.broadcast_to(inp["swa_sinks"][j][None, :], (128, 16))
    prm[:, PRM.EPSC] = EPS
    prm[:, PRM.ZERO] = 0.0
    prm[:, PRM.TINY] = 1e-30
    return prm


def _nsa_small(w2, pe):
    sm = np.zeros((128, 416), np.float32)
    w2k = w2[0].reshape(2, 128, 64)
    w2v = w2[1].reshape(2, 128, 64)
    for hc in range(2):
        sm[:, hc * 128:hc * 128 + 64] = w2k[hc]
        sm[:, hc * 128 + 64:hc * 128 + 128] = w2k[hc]
        sm[:, 256 + hc * 64:256 + hc * 64 + 64] = w2v[hc]
    sm[:, 384:416] = np.ascontiguousarray(pe.transpose(0, 2, 1)).reshape(128, 32)
    return sm


def build_program(wcols, nseq=2, nlayers=DEPTH, final_norm=True):
    nc = bass.Bass("TRN2", target_bir_lowering=False)
    x_d = nc.dram_tensor("xT", [2, KC, 128, S], F32, kind="ExternalInput").ap()
    y_d = nc.dram_tensor("yT", [2, KC, 128, S], F32, kind="ExternalOutput").ap()
    w_d = [nc.dram_tensor("w%d" % l, [128, wcols[l]], F32, kind="ExternalInput").ap() for l in range(DEPTH)]
    sm_d = nc.dram_tensor("nsasm", [2, 128, 416], F32, kind="ExternalInput").ap()
    prm_d = nc.dram_tensor("prm", [128, PRM.N], F32, kind="ExternalInput").ap()
    cos_d = nc.dram_tensor("cosT", [128, S], F32, kind="ExternalInput").ap()
    sin_d = nc.dram_tensor("sinT", [128, S], F32, kind="ExternalInput").ap()
    cbf_d = nc.dram_tensor("cbf", [128, CBF.N], F32, kind="ExternalInput").ap()
    cf_d = nc.dram_tensor("cf", [128, CF.N], F32, kind="ExternalInput").ap()

    from contextlib import ExitStack
    with ExitStack() as es:
        def sb(name, shape, dt):
            return es.enter_context(nc.sbuf_tensor(name, shape, dt))

        xT = sb("xT_sb", [128, KC, S], F32)
        hT = sb("hT_sb", [128, KC, S], BF16)
        cosT = sb("cos_sb", [128, S], F32)
        sinT = sb("sin_sb", [128, S], F32)
        prm = sb("prm_sb", [128, PRM.N], F32)
        cf = sb("cf_sb", [128, CF.N], F32)
        cbf = sb("cbf_sb", [128, CBF.N], BF16)
        wring = sb("wring", [128, NSLOT, SLOTC], BF16)
        nsm = sb("nsm", [128, 416], BF16)
        RA = sb("RA", [128, 16, 512], BF16)
        tf = sb("tf", [128, 6, 516], F32)
        ZQ = sb("ZQ", [128, 2, 2, 512], BF16)
        kcmpT = sb("kcmpT", [128, 4, 128], BF16)
        vcmp = sb("vcmp", [128, 4, 65], BF16)
        hidT = sb("hidT", [128, 4, 128], BF16)
        gtmp = sb("gtmp", [128, 4, 128], F32)
        peb = sb("peb", [128, 4], F32)
        Vt = sb("Vt", [128, NTT, 2, 65], BF16)
        gates = sb("gates", [128, NTT, 12], F32)
        Pcmp = sb("Pcmp", [128, 4, 512], BF16)
        PT = sb("PT", [128, NPT, 512], BF16)
        selT = sb("selT", [128, 512], BF16)
        st = sb("st", [128, 160], F32)
        sc = sb("sc", [128, 3, 32], F32)
        selm = sb("selm", [128, 32], BF16)
        opair = sb("opair", [128, 2, 128], BF16)
        otmp = sb("otmp", [128, 2, 64], F32)
        esk = sb("esk", [128, 16], F32)
        ps = es.enter_context(nc.psum_tensor("ps", [128, 8, 512], F32))
        psb = ps.bitcast(BF16)

        esem = {e: es.enter_context(nc.semaphore("s_" + e)) for e in ("pe", "act", "dve", "pool")}
        P = Prog(nc, esem)

        def ctr(name):
            return Ctr(es.enter_context(nc.semaphore(name)), name)

        c_const = ctr("c_const")
        c_x = ctr("c_x")
        c_out = [ctr("c_out0"), ctr("c_out1")]
        c_slot = [ctr("c_slot%d" % i) for i in range(NSLOT)]
        c_sm = ctr("c_sm")
        c_nsm = ctr("c_nsm")

        def mm(out, lhsT, rhs, start, stop, reads, writes):
            P.op("pe", lambda e: e.matmul(out, lhsT, rhs, start=start, stop=stop,
                                          skip_group_check=True), reads, writes)

        def tr(out, in_, ident, reads, writes):
            P.op("pe", lambda e: e.transpose(out, in_, ident), reads, writes)

        def act(out, in_, func, reads, writes, bias=None, scale=None):
            kw = {}
            if bias is not None:
                kw["bias"] = bias
            if scale is not None:
                kw["scale"] = scale
            P.op("act", lambda e: e.activation(out, in_, func, **kw), reads, writes)

        def ts(out, in0, s1, s2, op0, op1, reads, writes, eng="dve"):
            if op1 is None:
                P.op(eng, lambda e: e.tensor_scalar(out, in0, s1, None, op0), reads, writes)
            else:
                P.op(eng, lambda e: e.tensor_scalar(out, in0, s1, s2, op0, op1), reads, writes)

        def tt(out, in0, in1, op, reads, writes, eng="dve"):
            P.op(eng, lambda e: e.tensor_tensor(out, in0, in1, op), reads, writes)

        def stt(out, in0, scalar, in1, op0, op1, reads, writes, eng="dve"):
            P.op(eng, lambda e: e.scalar_tensor_tensor(out, in0, scalar, in1, op0, op1), reads, writes)

        def dma(q, out, in_, reads, writes, fill):
            P.op(q, lambda e: e.dma_start(out=out, in_=in_), reads, writes, fill=fill)

        bank_state = {"i": 0, "lo": 0}

        def nbank(hi=8):
            lo = bank_state["lo"]
            b = lo + bank_state["i"] % (hi - lo)
            bank_state["i"] += 1
            return b

        def pk(b):
            return ("ps", b)

        wstate = {"n": 0, "off": 0, "layer": 0}

        def wtile(ncols):
            slot = wstate["n"] % NSLOT
            wstate["n"] += 1
            l = wstate["layer"]
            off = wstate["off"]
            wstate["off"] += ncols
            dma("pool", wring[:, slot, 0:ncols], w_d[l][:, off:off + ncols], [], [("w", slot)], c_slot[slot])
            return wring[:, slot, :], ("w", slot)

        dma("sp", prm[:, :], prm_d[:, :], [], ["prm"], c_const)
        dma("sp", cf[:, :], cf_d[:, :], [], ["cf"], c_const)
        dma("sp", cosT[:, :], cos_d[:, :], [], ["cos"], c_const)
        dma("sp", sinT[:, :], sin_d[:, :], [], ["sin"], c_const)
        dma("pool", cbf[:, :], cbf_d[:, :], [], ["cbf"], c_sm)
        P.op("dve", lambda e: e.memset(selT[:, :], 0.0), [], ["selT"])
        P.op("dve", lambda e: e.memset(ZQ[:, :, :, :], 0.0), [], [("ZQ", 0), ("ZQ", 1)])

        def fill_zq(qT, Q):
            for pc in range(2):
                src = qT[:, pc * S + 512 * Q:pc * S + 512 * Q + 512]
                P.op("pool", lambda e, o=ZQ[0:64, pc, 0, :], i=src[0:64]: e.tensor_copy(o, i),
                     [("RA", pc * 4 + Q)], [("ZQ", pc)])
                P.op("pool", lambda e, o=ZQ[64:128, pc, 1, :], i=src[64:128]: e.tensor_copy(o, i),
                     [("RA", pc * 4 + Q)], [("ZQ", pc)])
        P.op("dve", lambda e: e.memset(vcmp[:, :, 64:65], 1.0), [], ["vcmp"])
        P.op("dve", lambda e: e.memset(Vt[:, :, :, 64:65], 1.0), [], ["Vt"])

        ident = cbf[:, CBF.ID:CBF.ID + 128]
        Mc = cbf[:, CBF.MC:CBF.MC + 128]
        Me = cbf[:, CBF.ME:CBF.ME + 128]
        McN = cbf[:, CBF.MCN:CBF.MCN + 128]
        MeN = cbf[:, CBF.MEN:CBF.MEN + 128]
        ones32 = cf[:, CF.ONES:CF.ONES + 128]
        onesb = cbf[:, CBF.ONESB:CBF.ONESB + 128]

        def xk(c, tb):
            return ("xT", c, tb)

        def hk(c, tb):
            return ("hT", c, tb)

        def tfk(i):
            return ("tf", i)

        def cols(tb):
            return slice(512 * tb, 512 * tb + 512)

        def norm(gcol, out_fn):
            for tb in range(NTB):
                b = nbank()
                for c in range(KC):
                    sq = tf[:, c % 2, :].bitcast(BF16)[:, 0:512]
                    act(sq, xT[:, c, cols(tb)], AF.Square, [xk(c, tb)], [tfk(c % 2)])
                    mm(ps[:, b, :], onesb, sq, c == 0, c == KC - 1, [tfk(c % 2), "cbf"], [pk(b)])
                rt = tf[:, 2, 0:512]
                act(rt, ps[:, b, :], AF.Sqrt, ["prm"], [pk(b), tfk(2)],
                    bias=prm[:, PRM.EPSC:PRM.EPSC + 1], scale=1.0 / D)
                rstd = tf[:, 3, 0:512]
                P.op("dve", lambda e, o=rstd, i=rt: e.reciprocal(o, i), [tfk(2)], [tfk(3)])
                for c in range(KC):
                    out_fn(c, tb, rstd, prm[:, gcol + c:gcol + c + 1])

        def norm_to_hT(gcol):
            def f(c, tb, rstd, g):
                stt(hT[:, c, cols(tb)], xT[:, c, cols(tb)], g, rstd, ALU.mult, ALU.mult,
                    [xk(c, tb), tfk(3), "prm"], [hk(c, tb)])
            norm(gcol, f)

        def proj(b, M, wt, wkey, coloff, ncol_tile, tb):
            for kc in range(KC):
                mm(ps[0:M, b, :], wt[:, kc * ncol_tile + coloff:kc * ncol_tile + coloff + M],
                   hT[:, kc, cols(tb)], kc == 0, kc == KC - 1, [wkey, hk(kc, tb)], [pk(b)])

        def rope_evac(out, okey, b1, b2, rows, tb, bias1, bias2):
            r0, r1 = rows
            t1 = tf[r0:r1, 4, 0:512]
            t2 = tf[r0:r1, 5, 0:512]
            stt(t1, ps[r0:r1, b1, :], bias1[r0:r1], cosT[r0:r1, cols(tb)], ALU.add, ALU.mult,
                ["cos", "prm"], [pk(b1), tfk(4)])
            stt(t2, ps[r0:r1, b2, :], bias2[r0:r1], sinT[r0:r1, cols(tb)], ALU.add, ALU.mult,
                ["sin", "prm"], [pk(b2), tfk(5)])
            tt(out, t1, t2, ALU.add, [tfk(4), tfk(5)], list(okey))

        zero_b = prm[:, PRM.ZERO:PRM.ZERO + 1]

        from collections import deque
        pend = deque()
        LA = 4
        ptc = {"i": 0}

        def push(fn):
            pend.append(fn)
            while len(pend) > LA:
                pend.popleft()()

        def flush():
            while pend:
                pend.popleft()()

        def attn_pair_Q(Q, pcz, branches, pre_pv=None):
            nbr = len(branches) + (1 if pre_pv is not None else 0)
            first = [True] * 4

            def acc_mm(i, r, lhsT, rhs, reads):
                ti = i - 4 * Q
                mm(ps[:, ti, r * 65:r * 65 + 65], lhsT, rhs, first[ti], True, reads, [pk(ti)])
                first[ti] = False

            if pre_pv is not None:
                def pre():
                    for hh in range(2):
                        for i in range(4 * Q, 4 * Q + 4):
                            pre_pv(hh, i, lambda lhsT, rhs, reads, i=i, hh=hh: acc_mm(i, hh * nbr, lhsT, rhs, reads))
                push(pre)
            br0 = 1 if pre_pv is not None else 0
            for bi, br in enumerate(branches):
                W = br["win"]
                jlo = 0 if W is None else max(0, 4 * Q - W)
                for j in range(jlo, 4 * Q + 4):
                    ilo = max(j, 4 * Q)
                    ihi = 4 * Q + 3 if W is None else min(j + W, 4 * Q + 3)
                    if ihi < ilo:
                        continue
                    c0 = 128 * (ilo - 4 * Q)
                    c1 = 128 * (ihi - 4 * Q + 1)
                    for hh in range(2):
                        rows = slice(64 * hh, 64 * hh + 64)
                        b = nbank()
                        mm(ps[:, b, c0:c1], br["kT"][:, 128 * j:128 * j + 128],
                           ZQ[:, pcz, hh, c0:c1], True, True,
                           [br["kkey"](j), ("ZQ", pcz)], [pk(b)])
                        pem = br.get("pemask", False)
                        if pem and j >= 4 * Q:
                            cc = 128 * (j - 4 * Q)
                            mm(ps[:, b, cc:cc + 128], ident, McN, False, True, ["cbf"], [pk(b)])
                        if pem and W is not None and 4 * Q <= j + W <= 4 * Q + 3:
                            cc = 128 * (j + W - 4 * Q)
                            mm(ps[:, b, cc:cc + 128], ident, MeN, False, True, ["cbf"], [pk(b)])
                        if br["sel"] and Q >= 2:
                            mm(ps[:, b, c0:c1], cbf[:, CBF.E + 128 * j:CBF.E + 128 * j + 128],
                               selT[:, c0:c1], False, True, ["cbf", "selT"], [pk(b)])
                        pi = ptc["i"] % NPT
                        ptc["i"] += 1
                        pt = PT[:, pi, :]
                        act(pt[:, c0:c1], ps[:, b, c0:c1], AF.Exp, [], [pk(b), ("PT", pi)], scale=SCALE)
                        if (not pem) and j >= 4 * Q:
                            cc = 128 * (j - 4 * Q)
                            tt(pt[:, cc:cc + 128], pt[:, cc:cc + 128], Mc, ALU.mult, ["cbf"], [("PT", pi)],
                               eng=("pool" if pi % 2 == 0 else "dve"))
                        if (not pem) and W is not None and 4 * Q <= j + W <= 4 * Q + 3:
                            cc = 128 * (j + W - 4 * Q)
                            tt(pt[:, cc:cc + 128], pt[:, cc:cc + 128], Me, ALU.mult, ["cbf"], [("PT", pi)],
                               eng=("pool" if pi % 2 == 0 else "dve"))

                        def pv(pt=pt, pi=pi, ilo=ilo, ihi=ihi, j=j, hh=hh, bi=bi, br=br):
                            for i in range(ilo, ihi + 1):
                                cc = 128 * (i - 4 * Q)
                                acc_mm(i, hh * nbr + br0 + bi, pt[:, cc:cc + 128], Vt[:, j, br["v"], :],
                                       [("PT", pi), "Vt"])
                        push(pv)

        def out_proj_partial(bias_col):
            wt, wkey = wtile(2048)
            for dm in range(KC):
                for tb in range(NTB):
                    b = nbank()
                    for pc in range(2):
                        mm(ps[:, b, :], wt[:, pc * 1024 + dm * 128:pc * 1024 + dm * 128 + 128],
                           RA[:, pc * 4 + tb, :], pc == 0, pc == 1, [wkey, ("RA", pc * 4 + tb)], [pk(b)])
                    if bias_col is None:
                        tt(xT[:, dm, cols(tb)], xT[:, dm, cols(tb)], ps[:, b, :], ALU.add,
                           [xk(dm, tb)], [pk(b), xk(dm, tb)])
                    else:
                        stt(xT[:, dm, cols(tb)], ps[:, b, :], prm[:, bias_col + dm:bias_col + dm + 1],
                            xT[:, dm, cols(tb)], ALU.add, ALU.add, [xk(dm, tb), "prm"], [pk(b), xk(dm, tb)])

        def evac_pair(Q, c, pcl, nbr, gate_fn, sink_cols):
            for ti in range(4):
                i = 4 * Q + ti
                nr = 2 * nbr
                den = st[:, 0:nr]
                acc = ps[:, ti, 0:nr * 65]
                if sink_cols is None:
                    ts(den, ps[:, ti, 64:nr * 65:65], 1e-30, None, ALU.max, None, [], [pk(ti), "st0"])
                else:
                    tt(den, ps[:, ti, 64:nr * 65:65], esk[:, sink_cols[0]:sink_cols[1]], ALU.add,
                       ["esk"], [pk(ti), "st0"])
                rc = st[:, 8:8 + nr]
                P.op("dve", lambda e, o=rc, a=den: e.reciprocal(o, a), ["st0"], ["st1"])
                if gate_fn is not None:
                    cf_ = st[:, 16:16 + nr]
                    tt(cf_, rc, gate_fn(i), ALU.mult, ["st1", "gates"], ["st2"])
                    ckey = "st2"
                else:
                    cf_ = rc
                    ckey = "st1"
                ob = (c + ti) % 2
                for hh in range(2):
                    dst = opair[:, ob, 64 * hh:64 * hh + 64]
                    if nbr == 1:
                        r = hh
                        ts(dst, ps[:, ti, r * 65:r * 65 + 64], cf_[:, r:r + 1], None, ALU.mult, None,
                           [ckey], [pk(ti), ("opair", ob)])
                    else:
                        tmp = otmp[:, hh, :]
                        r = hh * nbr
                        ts(tmp, ps[:, ti, r * 65:r * 65 + 64], cf_[:, r:r + 1], None, ALU.mult, None,
                           [ckey], [pk(ti), ("otmp", hh)])
                        for k in range(1, nbr):
                            r = hh * nbr + k
                            o_ = dst if k == nbr - 1 else tmp
                            wk = ("opair", ob) if k == nbr - 1 else ("otmp", hh)
                            stt(o_, ps[:, ti, r * 65:r * 65 + 64], cf_[:, r:r + 1], tmp, ALU.mult, ALU.add,
                                [ckey, ("otmp", hh)], [pk(ti), wk])
                b = nbank()
                tr(psb[:, b, 0:128], opair[:, ob, :], ident, [("opair", ob), "cbf"], [pk(b)])
                idx = pcl * 4 + Q
                act(RA[:, idx, 128 * ti:128 * ti + 128], psb[:, b, 0:128], AF.Copy, [],
                    [pk(b), ("RA", idx), ("RAv", idx)])

        def nsa_mixer(l, j):
            norm_to_hT(PRM.GMIX + 8 * l)
            if STAGE <= 0:
                return
            dma("pool", nsm[:, :], sm_d[j, :, :], [], ["nsm"], c_nsm)
            for g in range(4):
                wt, wkey = wtile(KC * 192)
                for tb in range(NTB):
                    b1 = nbank()
                    b2 = nbank()
                    proj(b1, 128, wt, wkey, 0, 192, tb)
                    proj(b2, 64, wt, wkey, 128, 192, tb)
                    idx = g * 4 + tb
                    rope_evac(RA[0:64, idx, :], [("RA", idx)], b1, b2, (0, 64), tb, zero_b, zero_b)
                    act(RA[64:128, idx, :], ps[64:128, b1, :], AF.Copy, [], [pk(b1), ("RAv", idx)])
            if STAGE <= 1:
                return
            w1a, w1ak = wtile(SLOTC)
            w1b, w1bk = wtile(SLOTC)

            def w1(rows, jj, hc):
                t = w1a if jj < 16 else w1b
                o = (jj % 16) * 256 + hc * 128
                return t[rows, o:o + 128], (w1ak if jj < 16 else w1bk)
            for kv in range(2):
                rows = slice(64 * kv, 64 * kv + 64)
                for hc in range(2):
                    b = nbank()
                    for jj in range(32):
                        lw, lk = w1(rows, jj, hc)
                        mm(ps[:, b, 0:1], lw, nsm[rows, 384 + jj:385 + jj], jj == 0, jj == 31,
                           [lk, "nsm"], [pk(b)])
                    act(peb[:, kv * 2 + hc:kv * 2 + hc + 1], ps[:, b, 0:1], AF.Copy, [], [pk(b), "peb"])
            kcv = RA[:, :, :].rearrange("p a b -> p (a b)")
            for g in range(4):
                for kv in range(2):
                    rows = slice(64 * kv, 64 * kv + 64)
                    rkeys = [("RA" if kv == 0 else "RAv", g * 4 + tb) for tb in range(NTB)]
                    for hc in range(2):
                        b = nbank()
                        for jj in range(32):
                            lw, lk = w1(rows, jj, hc)
                            mm(ps[:, b, 0:127], lw, kcv[rows, g * S + jj:g * S + jj + 16 * 126 + 1:16],
                               jj == 0, jj == 31, [lk] + rkeys, [pk(b)])
                        hi = kv * 2 + hc
                        z = gtmp[:, 0, 0:127]
                        act(z, ps[:, b, 0:127], AF.Identity, ["peb"], [pk(b), "gt0"], bias=peb[:, hi:hi + 1])
                        z2 = gtmp[:, 1, 0:127]
                        tt(z2, z, z, ALU.mult, ["gt0"], ["gt1"])
                        ts(z2, z2, 0.044715, 1.0, ALU.mult, ALU.add, ["gt1"], ["gt1"])
                        tt(z2, z2, z, ALU.mult, ["gt0", "gt1"], ["gt1"])
                        sg = gtmp[:, 2, 0:127]
                        act(sg, z2, AF.Sigmoid, ["gt1"], ["gt2"], scale=1.5957691216057308)
                        tt(hidT[:, hi, 0:127], sg, z, ALU.mult, ["gt0", "gt2"], [("hid", hi)])
                    b = nbank()
                    if kv == 0:
                        for hc in range(2):
                            mm(ps[:, b, 0:127], nsm[:, hc * 128:hc * 128 + 128], hidT[:, hc, 0:127],
                               hc == 0, hc == 1, ["nsm", ("hid", hc)], [pk(b)])
                        act(kcmpT[:, g, 0:127], ps[:, b, 0:127], AF.Copy, [], [pk(b), ("kcmp", g)])
                    else:
                        for hc in range(2):
                            mm(ps[0:127, b, 0:64], hidT[:, 2 + hc, 0:127], nsm[:, 256 + hc * 64:256 + hc * 64 + 64],
                               hc == 0, hc == 1, ["nsm", ("hid", 2 + hc)], [pk(b)])
                        act(vcmp[0:127, g, 0:64], ps[0:127, b, 0:64], AF.Copy, ["vcmp"], [pk(b), ("vcmpg", g)])
            if STAGE <= 2:
                return
            qT = RA[:, 0:8, :].rearrange("p a b -> p (a b)")
            ksT = RA[:, 8:12, :].rearrange("p a b -> p (a b)")
            kwT = RA[:, 12:16, :].rearrange("p a b -> p (a b)")
            for g in range(4):
                wt, wkey = wtile(SLOTC)
                for tb in range(NTB):
                    for which in range(2):
                        b1 = nbank()
                        b2 = nbank()
                        proj(b1, 128, wt, wkey, which * 256, 512, tb)
                        proj(b2, 128, wt, wkey, which * 256 + 128, 512, tb)
                        idx = 8 + which * 4 + tb
                        rope_evac(RA[:, idx, :], [("RA", idx), ("RAv", idx)], b1, b2, (0, 128), tb, zero_b, zero_b)
                wt, wkey = wtile(KC * 140)
                for i in range(NTT):
                    b = nbank()
                    for kc in range(KC):
                        mm(ps[:, b, 0:140], hT[:, kc, 128 * i:128 * i + 128], wt[:, kc * 140:kc * 140 + 140],
                           kc == 0, kc == KC - 1, [wkey, hk(kc, i // 4)], [pk(b)])
                    act(Vt[:, i, :, 0:64], ps[:, b, 0:128].rearrange("p (a d) -> p a d", a=2), AF.Copy,
                        [], [pk(b), "Vt"])
                    gs = st[:, 32:44]
                    tt(gs, ps[:, b, 128:140], prm[:, PRM.GB + 48 * j + 12 * g:PRM.GB + 48 * j + 12 * g + 12],
                       ALU.add, ["prm"], [pk(b), "st3"])
                    act(gates[:, i, :], gs, AF.Sigmoid, ["st3"], ["gates"])
                wt, wkey = wtile(SLOTC)
                for pc in range(2):
                    for tb in range(NTB):
                        b1 = nbank()
                        b2 = nbank()
                        proj(b1, 128, wt, wkey, pc * 256, 512, tb)
                        proj(b2, 128, wt, wkey, pc * 256 + 128, 512, tb)
                        idx = pc * 4 + tb
                        rope_evac(RA[:, idx, :], [("RA", idx), ("RAv", idx)], b1, b2, (0, 128), tb, zero_b, zero_b)
                if STAGE <= 3:
                    return
                bank_state["lo"] = 4
                for Q in range(NTB if STAGE > 4 else 1):
                    fill_zq(qT, Q)
                    for h in range(4):
                        pc, hh = h // 2, h % 2
                        b = nbank()
                        mm(ps[0:127, b, :], kcmpT[:, g, 0:127], ZQ[:, pc, hh, :],
                           True, False, [("kcmp", g), ("ZQ", pc)], [pk(b)])
                        mm(ps[0:127, b, :], cbf[:, CBF.ID:CBF.ID + 127],
                           cbf[:, CBF.MCMP + 512 * Q:CBF.MCMP + 512 * Q + 512], False, True, ["cbf"], [pk(b)])
                        act(Pcmp[0:127, h, :], ps[0:127, b, :], AF.Exp, [], [pk(b), ("Pcmp", h)], scale=SCALE)
                    if Q >= 2:
                        for ti in range(4):
                            i = 4 * Q + ti
                            b = nbank()
                            for h in range(4):
                                mm(ps[:, b, 33 * h:33 * h + 33], Pcmp[0:127, h, 128 * ti:128 * ti + 128],
                                   cbf[0:127, CBF.OVL:CBF.OVL + 33], h == 0, True, [("Pcmp", h), "cbf"], [pk(b)])
                            rc4 = st[:, 48:52]
                            P.op("dve", lambda e, o=rc4, a=ps[:, b, 32:132:33]: e.reciprocal(o, a), [], [pk(b), "st4"])
                            imp = sc[:, 0, :]
                            ts(imp, ps[:, b, 0:32], rc4[:, 0:1], None, ALU.mult, None, ["st4"], [pk(b), "sc0"])
                            for h in range(1, 4):
                                stt(imp, ps[:, b, 33 * h:33 * h + 32], rc4[:, h:h + 1], imp, ALU.mult, ALU.add,
                                    ["st4", "sc0"], [pk(b), "sc0"])
                            tt(imp, imp, cf[:, CF.CM + 32 * (i - 8):CF.CM + 32 * (i - 8) + 32], ALU.mult,
                               ["cf", "sc0"], ["sc0"])
                            tt(imp, imp, cf[:, CF.ADDC + 32 * (i - 8):CF.ADDC + 32 * (i - 8) + 32], ALU.add,
                               ["cf", "sc0"], ["sc0"])
                            m8 = st[:, 56:64]
                            P.op("dve", lambda e, o=m8, a=imp: e.max(o, a), ["sc0"], ["st5"])
                            s2 = sc[:, 1, :]
                            P.op("dve", lambda e, o=s2, r_=m8, a=imp: e.match_replace(o, r_, a, -1e9),
                                 ["sc0", "st5"], ["sc1"])
                            m8b = st[:, 64:72]
                            P.op("dve", lambda e, o=m8b, a=s2: e.max(o, a), ["sc1"], ["st6"])
                            thr = st[:, 72:73]
                            P.op("dve", lambda e, o=thr, a=m8b: e.tensor_reduce(o, a, AX.X, ALU.min), ["st6"], ["st7"])
                            ts(selm[:, :], imp, thr, -1.0, ALU.is_ge, ALU.add, ["sc0", "st7"], ["selm"])
                            b2 = nbank()
                            tr(psb[0:32, b2, 0:128], selm[:, :], ident, ["selm", "cbf"], [pk(b2)])
                            act(selT[0:32, 128 * ti:128 * ti + 128], psb[0:32, b2, 0:128], AF.Copy, [], [pk(b2), "selT"])
                    for pc in range(2):
                        c = 2 * g + pc

                        def pre_pv(hh, i, acc, pc=pc):
                            ti = i - 4 * Q
                            h = pc * 2 + hh
                            acc(Pcmp[0:127, h, 128 * ti:128 * ti + 128], vcmp[0:127, g, :],
                                [("Pcmp", h), ("vcmpg", g), "vcmp"])
                        branches = [
                            dict(kT=ksT, kkey=lambda jj: ("RA", 8 + jj // 4), v=0, win=None, sel=True),
                            dict(kT=kwT, kkey=lambda jj: ("RA", 12 + jj // 4), v=1, win=4, sel=False),
                        ]
                        attn_pair_Q(Q, pc, branches, pre_pv)
                        push(lambda Q=Q, c=c, pc=pc: evac_pair(Q, c, pc, 3, lambda i, pc=pc: gates[:, i, 6 * pc:6 * pc + 6], None))
                flush()
                bank_state["lo"] = 0
                out_proj_partial(None)

        def swa_mixer(l, j):
            norm_to_hT(PRM.GMIX + 8 * l)
            act(esk[:, :], prm[:, PRM.SK + 16 * j:PRM.SK + 16 * j + 16], AF.Exp, ["prm"], ["esk"])
            qT = RA[:, 0:8, :].rearrange("p a b -> p (a b)")
            kT = RA[:, 8:12, :].rearrange("p a b -> p (a b)")
            wt, wkey = wtile(KC * 128)
            for i in range(NTT):
                b = nbank()
                for kc in range(KC):
                    mm(ps[:, b, 0:128], hT[:, kc, 128 * i:128 * i + 128], wt[:, kc * 128:kc * 128 + 128],
                       kc == 0, kc == KC - 1, [wkey, hk(kc, i // 4)], [pk(b)])
                tt(Vt[:, i, :, 0:64], ps[:, b, 0:128].rearrange("p (a d) -> p a d", a=2),
                   prm[:, PRM.BV + 128 * j:PRM.BV + 128 * j + 128].rearrange("p (a d) -> p a d", a=2),
                   ALU.add, ["prm"], [pk(b), "Vt"])
            for g in range(2):
                wt, wkey = wtile(KC * 256)
                for tb in range(NTB):
                    b1 = nbank()
                    b2 = nbank()
                    proj(b1, 128, wt, wkey, 0, 256, tb)
                    proj(b2, 128, wt, wkey, 128, 256, tb)
                    idx = 8 + tb
                    rope_evac(RA[:, idx, :], [("RA", idx), ("RAv", idx)], b1, b2, (0, 128), tb,
                              prm[:, PRM.BK + 2 * j + g:PRM.BK + 2 * j + g + 1],
                              prm[:, PRM.BKR + 2 * j + g:PRM.BKR + 2 * j + g + 1])
                for pp in range(2):
                    wt, wkey = wtile(SLOTC)
                    for pc in range(2):
                        c = 4 * g + 2 * pp + pc
                        for tb in range(NTB):
                            b1 = nbank()
                            b2 = nbank()
                            proj(b1, 128, wt, wkey, pc * 256, 512, tb)
                            proj(b2, 128, wt, wkey, pc * 256 + 128, 512, tb)
                            idx = pc * 4 + tb
                            rope_evac(RA[:, idx, :], [("RA", idx), ("RAv", idx)], b1, b2, (0, 128), tb,
                                      prm[:, PRM.BQ + 8 * j + c:PRM.BQ + 8 * j + c + 1],
                                      prm[:, PRM.BQR + 8 * j + c:PRM.BQR + 8 * j + c + 1])
                        bank_state["lo"] = 4
                    for Q in range(NTB):
                        fill_zq(qT, Q)
                        for pc in range(2):
                            c = 4 * g + 2 * pp + pc
                            branches = [dict(kT=kT, kkey=lambda jj: ("RA", 8 + jj // 4), v=g, win=1, sel=False, pemask=True)]
                            attn_pair_Q(Q, pc, branches, None)
                            push(lambda Q=Q, c=c, pc=pc: evac_pair(Q, c, pc, 1, None, (2 * c, 2 * c + 2)))
                    flush()
                    bank_state["lo"] = 0
                    out_proj_partial(PRM.BO + 8 * j if (g == 0 and pp == 0) else None)

        def ffn(l):
            norm_to_hT(PRM.GFFN + 8 * l)
            actT = RA
            for g0 in range(0, NFC, GF):
                fcs = list(range(g0, min(g0 + GF, NFC)))
                wgu = {}
                for ii in range(0, len(fcs), 2):
                    wt, wkey = wtile(2048 * len(fcs[ii:ii + 2]))
                    for k2, f in enumerate(fcs[ii:ii + 2]):
                        wgu[f] = (wt, wkey, k2 * 2048)
                    for k2, f in enumerate(fcs[ii:ii + 2]):
                        fl = f - g0
                        wtt, wk, wo = wgu[f]
                        cw = PRM.CW + 66 * l + 3 * f
                        cb = PRM.CB + 22 * l + f
                        for tb in range(NTB):
                            ba = nbank()
                            bu = nbank()
                            for kc in range(KC):
                                mm(ps[:, ba, :], wtt[:, wo + kc * 256:wo + kc * 256 + 128], hT[:, kc, cols(tb)],
                                   kc == 0, kc == KC - 1, [wk, hk(kc, tb)], [pk(ba)])
                            for kc in range(KC):
                                mm(ps[:, bu, :], wtt[:, wo + kc * 256 + 128:wo + kc * 256 + 256], hT[:, kc, cols(tb)],
                                   kc == 0, kc == KC - 1, [wk, hk(kc, tb)], [pk(bu)])
                            ai = 4 + tb % 2
                            a_sb = tf[:, ai, :]
                            if tb == 0:
                                P.op("dve", lambda e, o=a_sb[:, 0:2]: e.memset(o, 0.0), [], [tfk(ai)])
                            else:
                                act(a_sb[:, 0:2], tf[:, 4 + (tb - 1) % 2, 512:514], AF.Copy,
                                    [tfk(4 + (tb - 1) % 2)], [tfk(ai)])
                            act(a_sb[:, 2:514], ps[:, ba, :], AF.Copy, [], [pk(ba), tfk(ai)])
                            y = tf[:, 0, 0:512]
                            ts(y, a_sb[:, 2:514], prm[:, cw + 2:cw + 3], prm[:, cb:cb + 1], ALU.mult, ALU.add,
                               [tfk(ai), "prm"], [tfk(0)])
                            stt(y, a_sb[:, 1:513], prm[:, cw + 1:cw + 2], y, ALU.mult, ALU.add,
                                [tfk(ai), "prm", tfk(0)], [tfk(0)])
                            stt(y, a_sb[:, 0:512], prm[:, cw:cw + 1], y, ALU.mult, ALU.add,
                                [tfk(ai), "prm", tfk(0)], [tfk(0)])
                            sl = tf[:, 1, 0:512]
                            act(sl, y, AF.Silu, [tfk(0)], [tfk(1)])
                            idx = fl * 4 + tb
                            tt(actT[:, idx, :], sl, ps[:, bu, :], ALU.mult, [tfk(1)], [pk(bu), ("RA", idx), ("RAv", idx)])
                wt, wkey = wtile(1024 * len(fcs))
                for dm in range(KC):
                    for tb in range(NTB):
                        b = nbank()
                        for fl in range(len(fcs)):
                            mm(ps[:, b, :], wt[:, fl * 1024 + dm * 128:fl * 1024 + dm * 128 + 128],
                               actT[:, fl * 4 + tb, :], fl == 0, fl == len(fcs) - 1, [wkey, ("RA", fl * 4 + tb)], [pk(b)])
                        tt(xT[:, dm, cols(tb)], xT[:, dm, cols(tb)], ps[:, b, :], ALU.add,
                           [xk(dm, tb)], [pk(b), xk(dm, tb)])

        for sq_ in range(nseq):
            for c in range(KC):
                dma("sp", xT[:, c, :], x_d[sq_, c, :, :], [], [xk(c, tb) for tb in range(NTB)], c_x)
            for l in range(nlayers):
                wstate["layer"] = l
                wstate["off"] = 0
                if l % 2 == 0:
                    nsa_mixer(l, l // 2)
                else:
                    swa_mixer(l, l // 2)
                if STAGE >= 10:
                    ffn(l)
            if final_norm:
                def fo(c, tb, rstd, g, sq_=sq_):
                    oi = (c + tb) % 2
                    o = tf[:, 4 + oi, 0:512]
                    stt(o, xT[:, c, cols(tb)], g, rstd, ALU.mult, ALU.mult,
                        [xk(c, tb), tfk(3), "prm"], [tfk(4 + oi)])
                    dma("sp", y_d[sq_, c, :, cols(tb)], o, [tfk(4 + oi)], [("y", sq_, c, tb)], c_out[oi])
                norm(PRM.GFIN, fo)
            else:
                for c in range(KC):
                    dma("sp", y_d[sq_, c, :, :], xT[:, c, :], [xk(c, tb) for tb in range(NTB)],
                        [("y", sq_, c, tb) for tb in range(NTB)], c_out[c % 2])
        P.op("sp", None, [("y", s_, c, tb) for s_ in range(nseq) for c in range(KC) for tb in range(NTB)], [])

        P.finalize()
        with nc.Block() as block:
            @block.tensor
            def _(e):
                P.emit_engine("pe", e)

            @block.scalar
            def _(e):
                P.emit_engine("act", e)

            @block.vector
            def _(e):
                P.emit_engine("dve", e)

            @block.gpsimd
            def _(e):
                P.emit_engine("pool", e)

            @block.sync
            def _(e):
                P.emit_engine("sp", e)
    return nc


def _prep(inp):
    inp = {k: np.asarray(v, dtype=np.float32) for k, v in inp.items()}
    streams = []
    for l in range(DEPTH):
        j = l // 2
        if l % 2 == 0:
            tiles = _nsa_stream(inp["nsa_w_in"][j], inp["nsa_cmp_w1"][j], inp["nsa_w_o"][j],
                                inp["ffn_w_gu"][l], inp["ffn_w_down"][l])
        else:
            tiles = _swa_stream(inp["swa_w_qkv"][j], inp["swa_w_o"][j], inp["ffn_w_gu"][l], inp["ffn_w_down"][l])
        streams.append(np.ascontiguousarray(np.concatenate(tiles, axis=1)))
    cosT, sinT, cbf, cf = _consts()
    shared = {"prm": _params(inp), "cosT": cosT, "sinT": sinT, "cbf": cbf, "cf": cf,
              "nsasm": np.stack([_nsa_small(inp["nsa_cmp_w2"][j], inp["nsa_cmp_pe"][j]) for j in range(2)])}
    for l in range(DEPTH):
        shared["w%d" % l] = streams[l]
    return inp, shared


def kernel(**inputs):
    inp, shared = _prep(inputs)
    x = inp["x"]
    nc = build_program([shared["w%d" % l].shape[1] for l in range(DEPTH)])
    in_maps = []
    for core in range(8):
        xs = x[2 * core:2 * core + 2]
        xT = np.ascontiguousarray(xs.transpose(0, 2, 1)).reshape(2, KC, 128, S)
        m = dict(shared)
        m["xT"] = xT
        in_maps.append(m)
    res = run_bass_kernel_spmd(nc, in_maps, core_ids=list(range(8)))
    out = np.empty((16, S, D), np.float32)
    for core in range(8):
        yT = res.results[core]["yT"].reshape(2, D, S)
        out[2 * core:2 * core + 2] = yT.transpose(0, 2, 1)
    return out
```

```python
import numpy as np
import concourse.bass as bass
import concourse.mybir as mybir
from concourse.bass_utils import run_bass_kernel_spmd

F32 = mybir.dt.float32
BF16 = mybir.dt.bfloat16
F32R = mybir.dt.float32r
AF = mybir.ActivationFunctionType
ALU = mybir.AluOpType
AX = mybir.AxisListType

S = 2048
D = 1024
KC = 8
NTB = 4
NTT = 16
DFF = 2816
NFC = 22
DEPTH = 4
BIG = 32768.0
SCALE = 0.125
EPS = 1e-6
NSLOT = 3
SLOTC = 4096
NPT = 6
GF = 4
STAGE = 99


class Rec:
    __slots__ = ("eng", "emit", "deps", "sig", "cnt", "fill", "ftarget")


class Ctr:
    def __init__(self, sem, name):
        self.sem = sem
        self.name = name
        self.count = 0


class Prog:
    ENG = ("pe", "act", "dve", "pool", "sp")

    def __init__(self, nc, esem):
        self.nc = nc
        self.esem = esem
        self.streams = {e: [] for e in self.ENG}
        self.last_w = {}
        self.readers = {}

    def op(self, eng, emit, reads=(), writes=(), fill=None):
        rec = Rec()
        rec.eng = eng
        rec.emit = emit
        rec.sig = False
        rec.cnt = None
        rec.fill = fill
        rec.ftarget = None
        deps = {}
        for k in reads:
            w = self.last_w.get(k)
            if w is not None:
                deps[id(w)] = w
        for k in writes:
            w = self.last_w.get(k)
            if w is not None:
                deps[id(w)] = w
            rd = self.readers.get(k)
            if rd:
                for r in rd.values():
                    deps[id(r)] = r
        for k in writes:
            self.last_w[k] = rec
            self.readers[k] = {}
        for k in reads:
            rk = eng if fill is None else ("dma", id(rec))
            self.readers.setdefault(k, {})[rk] = rec
        out = []
        for d in deps.values():
            if d is rec:
                continue
            if d.fill is None and d.eng == "pe" and eng == "pe" and fill is None:
                continue
            d.sig = True
            out.append((d, d.fill.count if d.fill is not None else None))
        rec.deps = out
        if fill is not None:
            fill.count += 16
            rec.ftarget = fill.count
        self.streams[eng].append(rec)
        return rec

    def finalize(self):
        for e in self.ENG:
            c = 0
            for r in self.streams[e]:
                if r.fill is None and r.sig:
                    c += 1
                    r.cnt = c

    def emit_engine(self, e, eng):
        waited = {}
        for r in self.streams[e]:
            ws = {}
            for d, snap in r.deps:
                if d.fill is not None:
                    nm, sem, val = d.fill.name, d.fill.sem, snap
                else:
                    nm, sem, val = d.eng, self.esem[d.eng], d.cnt
                if ws.get(nm, (None, 0))[1] < val:
                    ws[nm] = (sem, val)
            for nm, (sem, val) in ws.items():
                if waited.get(nm, 0) >= val:
                    continue
                waited[nm] = val
                eng.wait_ge(sem, val)
            if r.emit is not None:
                inst = r.emit(eng)
                if r.fill is not None:
                    inst.then_inc(r.fill.sem, 16)
                elif r.sig:
                    inst.then_inc(self.esem[e], 1)


def _rot(w, nheads):
    w4 = w.reshape(w.shape[0], nheads, 2, 32)
    return np.ascontiguousarray(w4[:, :, ::-1, :]).reshape(w.shape[0], nheads * 64)


def _tile_k(wc):
    n = wc.shape[1]
    return np.ascontiguousarray(wc.reshape(KC, 128, n).transpose(1, 0, 2)).reshape(128, KC * n)


def _ffn_tiles(w_gu, w_down):
    tiles = []
    for g0 in range(0, NFC, GF):
        fcs = list(range(g0, min(g0 + GF, NFC)))
        for i in range(0, len(fcs), 2):
            pr = fcs[i:i + 2]
            tiles.append(np.concatenate(
                [_tile_k(np.concatenate([w_gu[:, 128 * f:128 * f + 128],
                                         w_gu[:, DFF + 128 * f:DFF + 128 * f + 128]], axis=1))
                 for f in pr], axis=1))
        tiles.append(np.concatenate([w_down[128 * f:128 * f + 128, :] for f in fcs], axis=1))
    return tiles


def _nsa_stream(w_in, w1, w_o, w_gu, w_down):
    q = w_in[:, 0:1024]
    kc = w_in[:, 1024:1280]
    vc = w_in[:, 1280:1536]
    ks = w_in[:, 1536:1792]
    vs = w_in[:, 1792:2048]
    kw = w_in[:, 2048:2304]
    vw = w_in[:, 2304:2560]
    gt = w_in[:, 2560:2608]
    tiles = []
    for g in range(4):
        sl = slice(64 * g, 64 * g + 64)
        tiles.append(_tile_k(np.concatenate([kc[:, sl], vc[:, sl], _rot(kc[:, sl], 1)], axis=1)))
    w1t = np.ascontiguousarray(w1.reshape(2, 32, 64, 256).transpose(0, 2, 1, 3)).reshape(128, 32 * 256)
    tiles.append(w1t[:, 0:4096])
    tiles.append(w1t[:, 4096:8192])
    for g in range(4):
        sl = slice(64 * g, 64 * g + 64)
        ksg, kwg = ks[:, sl], kw[:, sl]
        tiles.append(_tile_k(np.concatenate(
            [ksg, ksg, _rot(ksg, 1), _rot(ksg, 1), kwg, kwg, _rot(kwg, 1), _rot(kwg, 1)], axis=1)))
        tiles.append(_tile_k(np.concatenate([vs[:, sl], vw[:, sl], gt[:, 12 * g:12 * g + 12]], axis=1)))
        qa = q[:, 256 * g:256 * g + 128]
        qb = q[:, 256 * g + 128:256 * g + 256]
        tiles.append(_tile_k(np.concatenate([qa, _rot(qa, 2), qb, _rot(qb, 2)], axis=1)))
        tiles.append(np.concatenate([w_o[256 * g:256 * g + 128, :], w_o[256 * g + 128:256 * g + 256, :]], axis=1))
    tiles += _ffn_tiles(w_gu, w_down)
    return tiles


def _swa_stream(w_qkv, w_o, w_gu, w_down):
    q = w_qkv[:, 0:1024]
    k = w_qkv[:, 1024:1152]
    v = w_qkv[:, 1152:1280]
    tiles = [_tile_k(v)]
    for g in range(2):
        kg = k[:, 64 * g:64 * g + 64]
        tiles.append(_tile_k(np.concatenate([kg, kg, _rot(kg, 1), _rot(kg, 1)], axis=1)))
        for pp in range(2):
            c0 = 4 * g + 2 * pp
            qa = q[:, 128 * c0:128 * c0 + 128]
            qb = q[:, 128 * c0 + 128:128 * c0 + 256]
            tiles.append(_tile_k(np.concatenate([qa, _rot(qa, 2), qb, _rot(qb, 2)], axis=1)))
            tiles.append(np.concatenate([w_o[128 * c0:128 * c0 + 128, :], w_o[128 * c0 + 128:128 * c0 + 256, :]], axis=1))
    tiles += _ffn_tiles(w_gu, w_down)
    return tiles


class PRM:
    GMIX = 0
    GFFN = 32
    GFIN = 64
    CW = 72
    CB = CW + 264
    GB = CB + 88
    BQ = GB + 96
    BQR = BQ + 16
    BK = BQR + 16
    BKR = BK + 4
    BO = BKR + 4
    BV = BO + 16
    SK = BV + 256
    EPSC = SK + 32
    ZERO = EPSC + 1
    TINY = ZERO + 1
    N = TINY + 1


class CBF:
    ID = 0
    MC = 128
    ME = 256
    MCMP = 384
    OVL = MCMP + 2048
    E = OVL + 40
    ONESB = E + 2048
    MCN = ONESB + 128
    MEN = MCN + 128
    N = MEN + 128


class CF:
    ONES = 0
    CM = 128
    ADDC = 128 + 256
    N = 128 + 512


def _consts():
    pos = np.arange(S, dtype=np.float32)
    inv = (10000.0 ** (-np.arange(0, 64, 2, dtype=np.float32) / 64.0)).astype(np.float32)
    ang = pos[:, None] * inv[None, :]
    cos = np.cos(ang).astype(np.float32).T
    sin = np.sin(ang).astype(np.float32).T
    cosT = np.concatenate([cos, cos, cos, cos], axis=0)
    sinT = np.concatenate([-sin, sin, -sin, sin], axis=0)
    cbf = np.zeros((128, CBF.N), np.float32)
    kk = np.arange(128)[:, None]
    qq = np.arange(128)[None, :]
    cbf[:, CBF.ID:CBF.ID + 128] = np.eye(128, dtype=np.float32)
    cbf[:, CBF.MC:CBF.MC + 128] = np.where(kk <= qq, 1.0, 0.0)
    cbf[:, CBF.ME:CBF.ME + 128] = np.where(kk > qq, 1.0, 0.0)
    c = np.arange(128)[:, None]
    t = np.arange(S)[None, :]
    cbf[:, CBF.MCMP:CBF.MCMP + S] = np.where(16 * c + 31 <= t, 0.0, -BIG)
    n = np.arange(32)[None, :]
    ovl = ((16 * c <= 64 * n + 63) & (16 * c + 31 >= 64 * n)).astype(np.float32)
    ovl[127, :] = 0.0
    cbf[:, CBF.OVL:CBF.OVL + 32] = ovl
    cbf[:, CBF.OVL + 32] = 1.0
    E = np.zeros((128, 16, 128), np.float32)
    for j in range(16):
        for k2 in range(128):
            E[2 * j + k2 // 64, j, k2] = BIG
    cbf[:, CBF.E:CBF.E + 2048] = E.reshape(128, 2048)
    cbf[:, CBF.ONESB:CBF.ONESB + 128] = 1.0
    cbf[:, CBF.MCN:CBF.MCN + 128] = np.where(kk <= qq, 0.0, -BIG)
    cbf[:, CBF.MEN:CBF.MEN + 128] = np.where(kk > qq, 0.0, -BIG)
    cf = np.zeros((128, CF.N), np.float32)
    cf[:, CF.ONES:CF.ONES + 128] = 1.0
    cm = np.zeros((128, 8, 32), np.float32)
    addc = np.zeros((128, 8, 32), np.float32)
    for i in range(8, 16):
        for p in range(128):
            cur = (128 * i + p) // 64
            for nb in range(32):
                if nb <= cur:
                    cm[p, i - 8, nb] = 1.0
                    if nb == 0 or nb == cur or nb == cur - 1:
                        addc[p, i - 8, nb] = 1e4
                else:
                    addc[p, i - 8, nb] = -1.0
    cf[:, CF.CM:CF.CM + 256] = cm.reshape(128, 256)
    cf[:, CF.ADDC:CF.ADDC + 256] = addc.reshape(128, 256)
    return cosT, sinT, cbf, cf


def _pp(v, n):
    return np.ascontiguousarray(v.reshape(n, 128).T)


def _params(inp):
    prm = np.zeros((128, PRM.N), np.float32)
    for l in range(4):
        prm[:, PRM.GMIX + 8 * l:PRM.GMIX + 8 * l + 8] = _pp(inp["norm_mix"][l], 8)
        prm[:, PRM.GFFN + 8 * l:PRM.GFFN + 8 * l + 8] = _pp(inp["norm_ffn"][l], 8)
        cw = inp["ffn_conv_w"][l]
        prm[:, PRM.CW + 66 * l:PRM.CW + 66 * l + 66] = np.ascontiguousarray(
            cw.reshape(3, NFC, 128).transpose(2, 1, 0)).reshape(128, 66)
        prm[:, PRM.CB + 22 * l:PRM.CB + 22 * l + 22] = _pp(inp["ffn_conv_b"][l], NFC)
    prm[:, PRM.GFIN:PRM.GFIN + 8] = _pp(inp["norm_final"], 8)
    for j in range(2):
        prm[:, PRM.GB + 48 * j:PRM.GB + 48 * j + 48] = np.broadcast_to(inp["nsa_gate_b"][j][None, :], (128, 48))
        b = inp["swa_b_qkv"][j]
        bq = b[0:1024]
        bk = b[1024:1152]
        bv = b[1152:1280]
        prm[:, PRM.BQ + 8 * j:PRM.BQ + 8 * j + 8] = _pp(bq, 8)
        prm[:, PRM.BQR + 8 * j:PRM.BQR + 8 * j + 8] = _pp(_rot(bq[None, :], 16)[0], 8)
        bkr = _rot(bk[None, :], 2)[0]
        for g in range(2):
            prm[:, PRM.BK + 2 * j + g] = np.concatenate([bk[64 * g:64 * g + 64]] * 2)
            prm[:, PRM.BKR + 2 * j + g] = np.concatenate([bkr[64 * g:64 * g + 64]] * 2)
        prm[:, PRM.BO + 8 * j:PRM.BO + 8 * j + 8] = _pp(inp["swa_b_o"][j], 8)
        prm[:, PRM.BV + 128 * j:PRM.BV + 128 * j + 128] = np.broadcast_to(bv[None, :], (128, 128))
        prm[:, PRM.SK + 16 * j:PRM.SK + 16 * j + 16] = np.broadcast_to(inp["swa_sinks"][j][None, :], (128, 16))
    prm[:, PRM.EPSC] = EPS
    prm[:, PRM.ZERO] = 0.0
    prm[:, PRM.TINY] = 1e-30
    return prm


def _nsa_small(w2, pe):
    sm = np.zeros((128, 416), np.float32)
    w2k = w2[0].reshape(2, 128, 64)
    w2v = w2[1].reshape(2, 128, 64)
    for hc in range(2):
        sm[:, hc * 128:hc * 128 + 64] = w2k[hc]
        sm[:, hc * 128 + 64:hc * 128 + 128] = w2k[hc]
        sm[:, 256 + hc * 64:256 + hc * 64 + 64] = w2v[hc]
    sm[:, 384:416] = np.ascontiguousarray(pe.transpose(0, 2, 1)).reshape(128, 32)
    return sm


def build_program(wcols, nseq=2, nlayers=DEPTH, final_norm=True):
    nc = bass.Bass("TRN2", target_bir_lowering=False)
    x_d = nc.dram_tensor("xT", [2, KC, 128, S], F32, kind="ExternalInput").ap()
    y_d = nc.dram_tensor("yT", [2, KC, 128, S], F32, kind="ExternalOutput").ap()
    w_d = [nc.dram_tensor("w%d" % l, [128, wcols[l]], F32, kind="ExternalInput").ap() for l in range(DEPTH)]
    sm_d = nc.dram_tensor("nsasm", [2, 128, 416], F32, kind="ExternalInput").ap()
    prm_d = nc.dram_tensor("prm", [128, PRM.N], F32, kind="ExternalInput").ap()
    cos_d = nc.dram_tensor("cosT", [128, S], F32, kind="ExternalInput").ap()
    sin_d = nc.dram_tensor("sinT", [128, S], F32, kind="ExternalInput").ap()
    cbf_d = nc.dram_tensor("cbf", [128, CBF.N], F32, kind="ExternalInput").ap()
    cf_d = nc.dram_tensor("cf", [128, CF.N], F32, kind="ExternalInput").ap()

    from contextlib import ExitStack
    with ExitStack() as es:
        def sb(name, shape, dt):
            return es.enter_context(nc.sbuf_tensor(name, shape, dt))

        xT = sb("xT_sb", [128, KC, S], F32)
        hT = sb("hT_sb", [128, KC, S], BF16)
        cosT = sb("cos_sb", [128, S], F32)
        sinT = sb("sin_sb", [128, S], F32)
        prm = sb("prm_sb", [128, PRM.N], F32)
        cf = sb("cf_sb", [128, CF.N], F32)
        cbf = sb("cbf_sb", [128, CBF.N], BF16)
        wring = sb("wring", [128, NSLOT, SLOTC], BF16)
        nsm = sb("nsm", [128, 416], BF16)
        RA = sb("RA", [128, 16, 512], BF16)
        tf = sb("tf", [128, 6, 516], F32)
        ZQ = sb("ZQ", [128, 2, 2, 512], BF16)
        kcmpT = sb("kcmpT", [128, 4, 128], BF16)
        vcmp = sb("vcmp", [128, 4, 65], BF16)
        hidT = sb("hidT", [128, 4, 128], BF16)
        gtmp = sb("gtmp", [128, 4, 128], F32)
        peb = sb("peb", [128, 4], F32)
        Vt = sb("Vt", [128, NTT, 2, 65], BF16)
        gates = sb("gates", [128, NTT, 12], F32)
        Pcmp = sb("Pcmp", [128, 4, 512], BF16)
        PT = sb("PT", [128, NPT, 512], BF16)
        selT = sb("selT", [128, 512], BF16)
        st = sb("st", [128, 160], F32)
        sc = sb("sc", [128, 3, 32], F32)
        selm = sb("selm", [128, 32], BF16)
        opair = sb("opair", [128, 2, 128], BF16)
        otmp = sb("otmp", [128, 2, 64], F32)
        esk = sb("esk", [128, 16], F32)
        ps = es.enter_context(nc.psum_tensor("ps", [128, 8, 512], F32))
        psb = ps.bitcast(BF16)

        esem = {e: es.enter_context(nc.semaphore("s_" + e)) for e in ("pe", "act", "dve", "pool")}
        P = Prog(nc, esem)

        def ctr(name):
            return Ctr(es.enter_context(nc.semaphore(name)), name)

        c_const = ctr("c_const")
        c_x = ctr("c_x")
        c_out = [ctr("c_out0"), ctr("c_out1")]
        c_slot = [ctr("c_slot%d" % i) for i in range(NSLOT)]
        c_sm = ctr("c_sm")
        c_nsm = ctr("c_nsm")

        def mm(out, lhsT, rhs, start, stop, reads, writes):
            P.op("pe", lambda e: e.matmul(out, lhsT, rhs, start=start, stop=stop,
                                          skip_group_check=True), reads, writes)

        def tr(out, in_, ident, reads, writes):
            P.op("pe", lambda e: e.transpose(out, in_, ident), reads, writes)

        def act(out, in_, func, reads, writes, bias=None, scale=None):
            kw = {}
            if bias is not None:
                kw["bias"] = bias
            if scale is not None:
                kw["scale"] = scale
            P.op("act", lambda e: e.activation(out, in_, func, **kw), reads, writes)

        def ts(out, in0, s1, s2, op0, op1, reads, writes, eng="dve"):
            if op1 is None:
                P.op(eng, lambda e: e.tensor_scalar(out, in0, s1, None, op0), reads, writes)
            else:
                P.op(eng, lambda e: e.tensor_scalar(out, in0, s1, s2, op0, op1), reads, writes)

        def tt(out, in0, in1, op, reads, writes, eng="dve"):
            P.op(eng, lambda e: e.tensor_tensor(out, in0, in1, op), reads, writes)

        def stt(out, in0, scalar, in1, op0, op1, reads, writes, eng="dve"):
            P.op(eng, lambda e: e.scalar_tensor_tensor(out, in0, scalar, in1, op0, op1), reads, writes)

        def dma(q, out, in_, reads, writes, fill):
            P.op(q, lambda e: e.dma_start(out=out, in_=in_), reads, writes, fill=fill)

        bank_state = {"i": 0, "lo": 0}

        def nbank(hi=8):
            lo = bank_state["lo"]
            b = lo + bank_state["i"] % (hi - lo)
            bank_state["i"] += 1
            return b

        def pk(b):
            return ("ps", b)

        wstate = {"n": 0, "off": 0, "layer": 0}

        def wtile(ncols):
            slot = wstate["n"] % NSLOT
            wstate["n"] += 1
            l = wstate["layer"]
            off = wstate["off"]
            wstate["off"] += ncols
            dma("pool", wring[:, slot, 0:ncols], w_d[l][:, off:off + ncols], [], [("w", slot)], c_slot[slot])
            return wring[:, slot, :], ("w", slot)

        dma("sp", prm[:, :], prm_d[:, :], [], ["prm"], c_const)
        dma("sp", cf[:, :], cf_d[:, :], [], ["cf"], c_const)
        dma("sp", cosT[:, :], cos_d[:, :], [], ["cos"], c_const)
        dma("sp", sinT[:, :], sin_d[:, :], [], ["sin"], c_const)
        dma("pool", cbf[:, :], cbf_d[:, :], [], ["cbf"], c_sm)
        P.op("dve", lambda e: e.memset(selT[:, :], 0.0), [], ["selT"])
        P.op("dve", lambda e: e.memset(ZQ[:, :, :, :], 0.0), [], [("ZQ", 0), ("ZQ", 1)])

        def fill_zq(qT, Q):
            for pc in range(2):
                src = qT[:, pc * S + 512 * Q:pc * S + 512 * Q + 512]
                P.op("pool", lambda e, o=ZQ[0:64, pc, 0, :], i=src[0:64]: e.tensor_copy(o, i),
                     [("RA", pc * 4 + Q)], [("ZQ", pc)])
                P.op("pool", lambda e, o=ZQ[64:128, pc, 1, :], i=src[64:128]: e.tensor_copy(o, i),
                     [("RA", pc * 4 + Q)], [("ZQ", pc)])
        P.op("dve", lambda e: e.memset(vcmp[:, :, 64:65], 1.0), [], ["vcmp"])
        P.op("dve", lambda e: e.memset(Vt[:, :, :, 64:65], 1.0), [], ["Vt"])

        ident = cbf[:, CBF.ID:CBF.ID + 128]
        Mc = cbf[:, CBF.MC:CBF.MC + 128]
        Me = cbf[:, CBF.ME:CBF.ME + 128]
        McN = cbf[:, CBF.MCN:CBF.MCN + 128]
        MeN = cbf[:, CBF.MEN:CBF.MEN + 128]
        ones32 = cf[:, CF.ONES:CF.ONES + 128]
        onesb = cbf[:, CBF.ONESB:CBF.ONESB + 128]

        def xk(c, tb):
            return ("xT", c, tb)

        def hk(c, tb):
            return ("hT", c, tb)

        def tfk(i):
            return ("tf", i)

        def cols(tb):
            return slice(512 * tb, 512 * tb + 512)

        def norm(gcol, out_fn):
            for tb in range(NTB):
                b = nbank()
                for c in range(KC):
                    sq = tf[:, c % 2, :].bitcast(BF16)[:, 0:512]
                    act(sq, xT[:, c, cols(tb)], AF.Square, [xk(c, tb)], [tfk(c % 2)])
                    mm(ps[:, b, :], onesb, sq, c == 0, c == KC - 1, [tfk(c % 2), "cbf"], [pk(b)])
                rt = tf[:, 2, 0:512]
                act(rt, ps[:, b, :], AF.Sqrt, ["prm"], [pk(b), tfk(2)],
                    bias=prm[:, PRM.EPSC:PRM.EPSC + 1], scale=1.0 / D)
                rstd = tf[:, 3, 0:512]
                P.op("dve", lambda e, o=rstd, i=rt: e.reciprocal(o, i), [tfk(2)], [tfk(3)])
                for c in range(KC):
                    out_fn(c, tb, rstd, prm[:, gcol + c:gcol + c + 1])

        def norm_to_hT(gcol):
            def f(c, tb, rstd, g):
                stt(hT[:, c, cols(tb)], xT[:, c, cols(tb)], g, rstd, ALU.mult, ALU.mult,
                    [xk(c, tb), tfk(3), "prm"], [hk(c, tb)])
            norm(gcol, f)

        def proj(b, M, wt, wkey, coloff, ncol_tile, tb):
            for kc in range(KC):
                mm(ps[0:M, b, :], wt[:, kc * ncol_tile + coloff:kc * ncol_tile + coloff + M],
                   hT[:, kc, cols(tb)], kc == 0, kc == KC - 1, [wkey, hk(kc, tb)], [pk(b)])

        def rope_evac(out, okey, b1, b2, rows, tb, bias1, bias2):
            r0, r1 = rows
            t1 = tf[r0:r1, 4, 0:512]
            t2 = tf[r0:r1, 5, 0:512]
            stt(t1, ps[r0:r1, b1, :], bias1[r0:r1], cosT[r0:r1, cols(tb)], ALU.add, ALU.mult,
                ["cos", "prm"], [pk(b1), tfk(4)])
            stt(t2, ps[r0:r1, b2, :], bias2[r0:r1], sinT[r0:r1, cols(tb)], ALU.add, ALU.mult,
                ["sin", "prm"], [pk(b2), tfk(5)])
            tt(out, t1, t2, ALU.add, [tfk(4), tfk(5)], list(okey))

        zero_b = prm[:, PRM.ZERO:PRM.ZERO + 1]

        from collections import deque
        pend = deque()
        LA = 4
        ptc = {"i": 0}

        def push(fn):
            pend.append(fn)
            while len(pend) > LA:
                pend.popleft()()

        def flush():
            while pend:
                pend.popleft()()

        def attn_pair_Q(Q, pcz, branches, pre_pv=None):
            nbr = len(branches) + (1 if pre_pv is not None else 0)
            first = [True] * 4

            def acc_mm(i, r, lhsT, rhs, reads):
                ti = i - 4 * Q
                mm(ps[:, ti, r * 65:r * 65 + 65], lhsT, rhs, first[ti], True, reads, [pk(ti)])
                first[ti] = False

            if pre_pv is not None:
                def pre():
                    for hh in range(2):
                        for i in range(4 * Q, 4 * Q + 4):
                            pre_pv(hh, i, lambda lhsT, rhs, reads, i=i, hh=hh: acc_mm(i, hh * nbr, lhsT, rhs, reads))
                push(pre)
            br0 = 1 if pre_pv is not None else 0
            for bi, br in enumerate(branches):
                W = br["win"]
                jlo = 0 if W is None else max(0, 4 * Q - W)
                for j in range(jlo, 4 * Q + 4):
                    ilo = max(j, 4 * Q)
                    ihi = 4 * Q + 3 if W is None else min(j + W, 4 * Q + 3)
                    if ihi < ilo:
                        continue
                    c0 = 128 * (ilo - 4 * Q)
                    c1 = 128 * (ihi - 4 * Q + 1)
                    for hh in range(2):
                        rows = slice(64 * hh, 64 * hh + 64)
                        b = nbank()
                        mm(ps[:, b, c0:c1], br["kT"][:, 128 * j:128 * j + 128],
                           ZQ[:, pcz, hh, c0:c1], True, True,
                           [br["kkey"](j), ("ZQ", pcz)], [pk(b)])
                        pem = br.get("pemask", False)
                        if pem and j >= 4 * Q:
                            cc = 128 * (j - 4 * Q)
                            mm(ps[:, b, cc:cc + 128], ident, McN, False, True, ["cbf"], [pk(b)])
                        if pem and W is not None and 4 * Q <= j + W <= 4 * Q + 3:
                            cc = 128 * (j + W - 4 * Q)
                            mm(ps[:, b, cc:cc + 128], ident, MeN, False, True, ["cbf"], [pk(b)])
                        if br["sel"] and Q >= 2:
                            mm(ps[:, b, c0:c1], cbf[:, CBF.E + 128 * j:CBF.E + 128 * j + 128],
                               selT[:, c0:c1], False, True, ["cbf", "selT"], [pk(b)])
                        pi = ptc["i"] % NPT
                        ptc["i"] += 1
                        pt = PT[:, pi, :]
                        act(pt[:, c0:c1], ps[:, b, c0:c1], AF.Exp, [], [pk(b), ("PT", pi)], scale=SCALE)
                        if (not pem) and j >= 4 * Q:
                            cc = 128 * (j - 4 * Q)
                            tt(pt[:, cc:cc + 128], pt[:, cc:cc + 128], Mc, ALU.mult, ["cbf"], [("PT", pi)],
                               eng=("pool" if pi % 2 == 0 else "dve"))
                        if (not pem) and W is not None and 4 * Q <= j + W <= 4 * Q + 3:
                            cc = 128 * (j + W - 4 * Q)
                            tt(pt[:, cc:cc + 128], pt[:, cc:cc + 128], Me, ALU.mult, ["cbf"], [("PT", pi)],
                               eng=("pool" if pi % 2 == 0 else "dve"))

                        def pv(pt=pt, pi=pi, ilo=ilo, ihi=ihi, j=j, hh=hh, bi=bi, br=br):
                            for i in range(ilo, ihi + 1):
                                cc = 128 * (i - 4 * Q)
                                acc_mm(i, hh * nbr + br0 + bi, pt[:, cc:cc + 128], Vt[:, j, br["v"], :],
                                       [("PT", pi), "Vt"])
                        push(pv)

        def out_proj_partial(bias_col):
            wt, wkey = wtile(2048)
            for dm in range(KC):
                for tb in range(NTB):
                    b = nbank()
                    for pc in range(2):
                        mm(ps[:, b, :], wt[:, pc * 1024 + dm * 128:pc * 1024 + dm * 128 + 128],
                           RA[:, pc * 4 + tb, :], pc == 0, pc == 1, [wkey, ("RA", pc * 4 + tb)], [pk(b)])
                    if bias_col is None:
                        tt(xT[:, dm, cols(tb)], xT[:, dm, cols(tb)], ps[:, b, :], ALU.add,
                           [xk(dm, tb)], [pk(b), xk(dm, tb)])
                    else:
                        stt(xT[:, dm, cols(tb)], ps[:, b, :], prm[:, bias_col + dm:bias_col + dm + 1],
                            xT[:, dm, cols(tb)], ALU.add, ALU.add, [xk(dm, tb), "prm"], [pk(b), xk(dm, tb)])

        def evac_pair(Q, c, pcl, nbr, gate_fn, sink_cols):
            for ti in range(4):
                i = 4 * Q + ti
                nr = 2 * nbr
                den = st[:, 0:nr]
                acc = ps[:, ti, 0:nr * 65]
                if sink_cols is None:
                    ts(den, ps[:, ti, 64:nr * 65:65], 1e-30, None, ALU.max, None, [], [pk(ti), "st0"])
                else:
                    tt(den, ps[:, ti, 64:nr * 65:65], esk[:, sink_cols[0]:sink_cols[1]], ALU.add,
                       ["esk"], [pk(ti), "st0"])
                rc = st[:, 8:8 + nr]
                P.op("dve", lambda e, o=rc, a=den: e.reciprocal(o, a), ["st0"], ["st1"])
                if gate_fn is not None:
                    cf_ = st[:, 16:16 + nr]
                    tt(cf_, rc, gate_fn(i), ALU.mult, ["st1", "gates"], ["st2"])
                    ckey = "st2"
                else:
                    cf_ = rc
                    ckey = "st1"
                ob = (c + ti) % 2
                for hh in range(2):
                    dst = opair[:, ob, 64 * hh:64 * hh + 64]
                    if nbr == 1:
                        r = hh
                        ts(dst, ps[:, ti, r * 65:r * 65 + 64], cf_[:, r:r + 1], None, ALU.mult, None,
                           [ckey], [pk(ti), ("opair", ob)])
                    else:
                        tmp = otmp[:, hh, :]
                        r = hh * nbr
                        ts(tmp, ps[:, ti, r * 65:r * 65 + 64], cf_[:, r:r + 1], None, ALU.mult, None,
                           [ckey], [pk(ti), ("otmp", hh)])
                        for k in range(1, nbr):
                            r = hh * nbr + k
                            o_ = dst if k == nbr - 1 else tmp
                            wk = ("opair", ob) if k == nbr - 1 else ("otmp", hh)
                            stt(o_, ps[:, ti, r * 65:r * 65 + 64], cf_[:, r:r + 1], tmp, ALU.mult, ALU.add,
                                [ckey, ("otmp", hh)], [pk(ti), wk])
                b = nbank()
                tr(psb[:, b, 0:128], opair[:, ob, :], ident, [("opair", ob), "cbf"], [pk(b)])
                idx = pcl * 4 + Q
                act(RA[:, idx, 128 * ti:128 * ti + 128], psb[:, b, 0:128], AF.Copy, [],
                    [pk(b), ("RA", idx), ("RAv", idx)])

        def nsa_mixer(l, j):
            norm_to_hT(PRM.GMIX + 8 * l)
            if STAGE <= 0:
                return
            dma("pool", nsm[:, :], sm_d[j, :, :], [], ["nsm"], c_nsm)
            for g in range(4):
                wt, wkey = wtile(KC * 192)
                for tb in range(NTB):
                    b1 = nbank()
                    b2 = nbank()
                    proj(b1, 128, wt, wkey, 0, 192, tb)
                    proj(b2, 64, wt, wkey, 128, 192, tb)
                    idx = g * 4 + tb
                    rope_evac(RA[0:64, idx, :], [("RA", idx)], b1, b2, (0, 64), tb, zero_b, zero_b)
                    act(RA[64:128, idx, :], ps[64:128, b1, :], AF.Copy, [], [pk(b1), ("RAv", idx)])
            if STAGE <= 1:
                return
            w1a, w1ak = wtile(SLOTC)
            w1b, w1bk = wtile(SLOTC)

            def w1(rows, jj, hc):
                t = w1a if jj < 16 else w1b
                o = (jj % 16) * 256 + hc * 128
                return t[rows, o:o + 128], (w1ak if jj < 16 else w1bk)
            for kv in range(2):
                rows = slice(64 * kv, 64 * kv + 64)
                for hc in range(2):
                    b = nbank()
                    for jj in range(32):
                        lw, lk = w1(rows, jj, hc)
                        mm(ps[:, b, 0:1], lw, nsm[rows, 384 + jj:385 + jj], jj == 0, jj == 31,
                           [lk, "nsm"], [pk(b)])
                    act(peb[:, kv * 2 + hc:kv * 2 + hc + 1], ps[:, b, 0:1], AF.Copy, [], [pk(b), "peb"])
            kcv = RA[:, :, :].rearrange("p a b -> p (a b)")
            for g in range(4):
                for kv in range(2):
                    rows = slice(64 * kv, 64 * kv + 64)
                    rkeys = [("RA" if kv == 0 else "RAv", g * 4 + tb) for tb in range(NTB)]
                    for hc in range(2):
                        b = nbank()
                        for jj in range(32):
                            lw, lk = w1(rows, jj, hc)
                            mm(ps[:, b, 0:127], lw, kcv[rows, g * S + jj:g * S + jj + 16 * 126 + 1:16],
                               jj == 0, jj == 31, [lk] + rkeys, [pk(b)])
                        hi = kv * 2 + hc
                        z = gtmp[:, 0, 0:127]
                        act(z, ps[:, b, 0:127], AF.Identity, ["peb"], [pk(b), "gt0"], bias=peb[:, hi:hi + 1])
                        z2 = gtmp[:, 1, 0:127]
                        tt(z2, z, z, ALU.mult, ["gt0"], ["gt1"])
                        ts(z2, z2, 0.044715, 1.0, ALU.mult, ALU.add, ["gt1"], ["gt1"])
                        tt(z2, z2, z, ALU.mult, ["gt0", "gt1"], ["gt1"])
                        sg = gtmp[:, 2, 0:127]
                        act(sg, z2, AF.Sigmoid, ["gt1"], ["gt2"], scale=1.5957691216057308)
                        tt(hidT[:, hi, 0:127], sg, z, ALU.mult, ["gt0", "gt2"], [("hid", hi)])
                    b = nbank()
                    if kv == 0:
                        for hc in range(2):
                            mm(ps[:, b, 0:127], nsm[:, hc * 128:hc * 128 + 128], hidT[:, hc, 0:127],
                               hc == 0, hc == 1, ["nsm", ("hid", hc)], [pk(b)])
                        act(kcmpT[:, g, 0:127], ps[:, b, 0:127], AF.Copy, [], [pk(b), ("kcmp", g)])
                    else:
                        for hc in range(2):
                            mm(ps[0:127, b, 0:64], hidT[:, 2 + hc, 0:127], nsm[:, 256 + hc * 64:256 + hc * 64 + 64],
                               hc == 0, hc == 1, ["nsm", ("hid", 2 + hc)], [pk(b)])
                        act(vcmp[0:127, g, 0:64], ps[0:127, b, 0:64], AF.Copy, ["vcmp"], [pk(b), ("vcmpg", g)])
            if STAGE <= 2:
                return
            qT = RA[:, 0:8, :].rearrange("p a b -> p (a b)")
            ksT = RA[:, 8:12, :].rearrange("p a b -> p (a b)")
            kwT = RA[:, 12:16, :].rearrange("p a b -> p (a b)")
            for g in range(4):
                wt, wkey = wtile(SLOTC)
                for tb in range(NTB):
                    for which in range(2):
                        b1 = nbank()
                        b2 = nbank()
                        proj(b1, 128, wt, wkey, which * 256, 512, tb)
                        proj(b2, 128, wt, wkey, which * 256 + 128, 512, tb)
                        idx = 8 + which * 4 + tb
                        rope_evac(RA[:, idx, :], [("RA", idx), ("RAv", idx)], b1, b2, (0, 128), tb, zero_b, zero_b)
                wt, wkey = wtile(KC * 140)
                for i in range(NTT):
                    b = nbank()
                    for kc in range(KC):
                        mm(ps[:, b, 0:140], hT[:, kc, 128 * i:128 * i + 128], wt[:, kc * 140:kc * 140 + 140],
                           kc == 0, kc == KC - 1, [wkey, hk(kc, i // 4)], [pk(b)])
                    act(Vt[:, i, :, 0:64], ps[:, b, 0:128].rearrange("p (a d) -> p a d", a=2), AF.Copy,
                        [], [pk(b), "Vt"])
                    gs = st[:, 32:44]
                    tt(gs, ps[:, b, 128:140], prm[:, PRM.GB + 48 * j + 12 * g:PRM.GB + 48 * j + 12 * g + 12],
                       ALU.add, ["prm"], [pk(b), "st3"])
                    act(gates[:, i, :], gs, AF.Sigmoid, ["st3"], ["gates"])
                wt, wkey = wtile(SLOTC)
                for pc in range(2):
                    for tb in range(NTB):
                        b1 = nbank()
                        b2 = nbank()
                        proj(b1, 128, wt, wkey, pc * 256, 512, tb)
                        proj(b2, 128, wt, wkey, pc * 256 + 128, 512, tb)
                        idx = pc * 4 + tb
                        rope_evac(RA[:, idx, :], [("RA", idx), ("RAv", idx)], b1, b2, (0, 128), tb, zero_b, zero_b)
                if STAGE <= 3:
                    return
                bank_state["lo"] = 4
                for Q in range(NTB if STAGE > 4 else 1):
                    fill_zq(qT, Q)
                    for h in range(4):
                        pc, hh = h // 2, h % 2
                        b = nbank()
                        mm(ps[0:127, b, :], kcmpT[:, g, 0:127], ZQ[:, pc, hh, :],
                           True, False, [("kcmp", g), ("ZQ", pc)], [pk(b)])
                        mm(ps[0:127, b, :], cbf[:, CBF.ID:CBF.ID + 127],
                           cbf[:, CBF.MCMP + 512 * Q:CBF.MCMP + 512 * Q + 512], False, True, ["cbf"], [pk(b)])
                        act(Pcmp[0:127, h, :], ps[0:127, b, :], AF.Exp, [], [pk(b), ("Pcmp", h)], scale=SCALE)
                    if Q >= 2:
                        for ti in range(4):
                            i = 4 * Q + ti
                            b = nbank()
                            for h in range(4):
                                mm(ps[:, b, 33 * h:33 * h + 33], Pcmp[0:127, h, 128 * ti:128 * ti + 128],
                                   cbf[0:127, CBF.OVL:CBF.OVL + 33], h == 0, True, [("Pcmp", h), "cbf"], [pk(b)])
                            rc4 = st[:, 48:52]
                            P.op("dve", lambda e, o=rc4, a=ps[:, b, 32:132:33]: e.reciprocal(o, a), [], [pk(b), "st4"])
                            imp = sc[:, 0, :]
                            ts(imp, ps[:, b, 0:32], rc4[:, 0:1], None, ALU.mult, None, ["st4"], [pk(b), "sc0"])
                            for h in range(1, 4):
                                stt(imp, ps[:, b, 33 * h:33 * h + 32], rc4[:, h:h + 1], imp, ALU.mult, ALU.add,
                                    ["st4", "sc0"], [pk(b), "sc0"])
                            tt(imp, imp, cf[:, CF.CM + 32 * (i - 8):CF.CM + 32 * (i - 8) + 32], ALU.mult,
                               ["cf", "sc0"], ["sc0"])
                            tt(imp, imp, cf[:, CF.ADDC + 32 * (i - 8):CF.ADDC + 32 * (i - 8) + 32], ALU.add,
                               ["cf", "sc0"], ["sc0"])
                            m8 = st[:, 56:64]
                            P.op("dve", lambda e, o=m8, a=imp: e.max(o, a), ["sc0"], ["st5"])
                            s2 = sc[:, 1, :]
                            P.op("dve", lambda e, o=s2, r_=m8, a=imp: e.match_replace(o, r_, a, -1e9),
                                 ["sc0", "st5"], ["sc1"])
                            m8b = st[:, 64:72]
                            P.op("dve", lambda e, o=m8b, a=s2: e.max(o, a), ["sc1"], ["st6"])
                            thr = st[:, 72:73]
                            P.op("dve", lambda e, o=thr, a=m8b: e.tensor_reduce(o, a, AX.X, ALU.min), ["st6"], ["st7"])
                            ts(selm[:, :], imp, thr, -1.0, ALU.is_ge, ALU.add, ["sc0", "st7"], ["selm"])
                            b2 = nbank()
                            tr(psb[0:32, b2, 0:128], selm[:, :], ident, ["selm", "cbf"], [pk(b2)])
                            act(selT[0:32, 128 * ti:128 * ti + 128], psb[0:32, b2, 0:128], AF.Copy, [], [pk(b2), "selT"])
                    for pc in range(2):
                        c = 2 * g + pc

                        def pre_pv(hh, i, acc, pc=pc):
                            ti = i - 4 * Q
                            h = pc * 2 + hh
                            acc(Pcmp[0:127, h, 128 * ti:128 * ti + 128], vcmp[0:127, g, :],
                                [("Pcmp", h), ("vcmpg", g), "vcmp"])
                        branches = [
                            dict(kT=ksT, kkey=lambda jj: ("RA", 8 + jj // 4), v=0, win=None, sel=True, pemask=True),
                            dict(kT=kwT, kkey=lambda jj: ("RA", 12 + jj // 4), v=1, win=4, sel=False, pemask=True),
                        ]
                        attn_pair_Q(Q, pc, branches, pre_pv)
                        push(lambda Q=Q, c=c, pc=pc: evac_pair(Q, c, pc, 3, lambda i, pc=pc: gates[:, i, 6 * pc:6 * pc + 6], None))
                flush()
                bank_state["lo"] = 0
                out_proj_partial(None)

        def swa_mixer(l, j):
            norm_to_hT(PRM.GMIX + 8 * l)
            act(esk[:, :], prm[:, PRM.SK + 16 * j:PRM.SK + 16 * j + 16], AF.Exp, ["prm"], ["esk"])
            qT = RA[:, 0:8, :].rearrange("p a b -> p (a b)")
            kT = RA[:, 8:12, :].rearrange("p a b -> p (a b)")
            wt, wkey = wtile(KC * 128)
            for i in range(NTT):
                b = nbank()
                for kc in range(KC):
                    mm(ps[:, b, 0:128], hT[:, kc, 128 * i:128 * i + 128], wt[:, kc * 128:kc * 128 + 128],
                       kc == 0, kc == KC - 1, [wkey, hk(kc, i // 4)], [pk(b)])
                tt(Vt[:, i, :, 0:64], ps[:, b, 0:128].rearrange("p (a d) -> p a d", a=2),
                   prm[:, PRM.BV + 128 * j:PRM.BV + 128 * j + 128].rearrange("p (a d) -> p a d", a=2),
                   ALU.add, ["prm"], [pk(b), "Vt"])
            for g in range(2):
                wt, wkey = wtile(KC * 256)
                for tb in range(NTB):
                    b1 = nbank()
                    b2 = nbank()
                    proj(b1, 128, wt, wkey, 0, 256, tb)
                    proj(b2, 128, wt, wkey, 128, 256, tb)
                    idx = 8 + tb
                    rope_evac(RA[:, idx, :], [("RA", idx), ("RAv", idx)], b1, b2, (0, 128), tb,
                              prm[:, PRM.BK + 2 * j + g:PRM.BK + 2 * j + g + 1],
                              prm[:, PRM.BKR + 2 * j + g:PRM.BKR + 2 * j + g + 1])
                for pp in range(2):
                    wt, wkey = wtile(SLOTC)
                    for pc in range(2):
                        c = 4 * g + 2 * pp + pc
                        for tb in range(NTB):
                            b1 = nbank()
                            b2 = nbank()
                            proj(b1, 128, wt, wkey, pc * 256, 512, tb)
                            proj(b2, 128, wt, wkey, pc * 256 + 128, 512, tb)
                            idx = pc * 4 + tb
                            rope_evac(RA[:, idx, :], [("RA", idx), ("RAv", idx)], b1, b2, (0, 128), tb,
                                      prm[:, PRM.BQ + 8 * j + c:PRM.BQ + 8 * j + c + 1],
                                      prm[:, PRM.BQR + 8 * j + c:PRM.BQR + 8 * j + c + 1])
                        bank_state["lo"] = 4
                    for Q in range(NTB):
                        fill_zq(qT, Q)
                        for pc in range(2):
                            c = 4 * g + 2 * pp + pc
                            branches = [dict(kT=kT, kkey=lambda jj: ("RA", 8 + jj // 4), v=g, win=1, sel=False, pemask=True)]
                            attn_pair_Q(Q, pc, branches, None)
                            push(lambda Q=Q, c=c, pc=pc: evac_pair(Q, c, pc, 1, None, (2 * c, 2 * c + 2)))
                    flush()
                    bank_state["lo"] = 0
                    out_proj_partial(PRM.BO + 8 * j if (g == 0 and pp == 0) else None)

        def ffn(l):
            norm_to_hT(PRM.GFFN + 8 * l)
            actT = RA
            for g0 in range(0, NFC, GF):
                fcs = list(range(g0, min(g0 + GF, NFC)))
                wgu = {}
                for ii in range(0, len(fcs), 2):
                    wt, wkey = wtile(2048 * len(fcs[ii:ii + 2]))
                    for k2, f in enumerate(fcs[ii:ii + 2]):
                        wgu[f] = (wt, wkey, k2 * 2048)
                    for k2, f in enumerate(fcs[ii:ii + 2]):
                        fl = f - g0
                        wtt, wk, wo = wgu[f]
                        cw = PRM.CW + 66 * l + 3 * f
                        cb = PRM.CB + 22 * l + f
                        for tb in range(NTB):
                            ba = nbank()
                            bu = nbank()
                            for kc in range(KC):
                                mm(ps[:, ba, :], wtt[:, wo + kc * 256:wo + kc * 256 + 128], hT[:, kc, cols(tb)],
                                   kc == 0, kc == KC - 1, [wk, hk(kc, tb)], [pk(ba)])
                            for kc in range(KC):
                                mm(ps[:, bu, :], wtt[:, wo + kc * 256 + 128:wo + kc * 256 + 256], hT[:, kc, cols(tb)],
                                   kc == 0, kc == KC - 1, [wk, hk(kc, tb)], [pk(bu)])
                            ai = 4 + tb % 2
                            a_sb = tf[:, ai, :]
                            if tb == 0:
                                P.op("dve", lambda e, o=a_sb[:, 0:2]: e.memset(o, 0.0), [], [tfk(ai)])
                            else:
                                act(a_sb[:, 0:2], tf[:, 4 + (tb - 1) % 2, 512:514], AF.Copy,
                                    [tfk(4 + (tb - 1) % 2)], [tfk(ai)])
                            act(a_sb[:, 2:514], ps[:, ba, :], AF.Copy, [], [pk(ba), tfk(ai)])
                            y = tf[:, 0, 0:512]
                            ts(y, a_sb[:, 2:514], prm[:, cw + 2:cw + 3], prm[:, cb:cb + 1], ALU.mult, ALU.add,
                               [tfk(ai), "prm"], [tfk(0)])
                            stt(y, a_sb[:, 1:513], prm[:, cw + 1:cw + 2], y, ALU.mult, ALU.add,
                                [tfk(ai), "prm", tfk(0)], [tfk(0)])
                            stt(y, a_sb[:, 0:512], prm[:, cw:cw + 1], y, ALU.mult, ALU.add,
                                [tfk(ai), "prm", tfk(0)], [tfk(0)])
                            sl = tf[:, 1, 0:512]
                            act(sl, y, AF.Silu, [tfk(0)], [tfk(1)])
                            idx = fl * 4 + tb
                            tt(actT[:, idx, :], sl, ps[:, bu, :], ALU.mult, [tfk(1)], [pk(bu), ("RA", idx), ("RAv", idx)])
                wt, wkey = wtile(1024 * len(fcs))
                for dm in range(KC):
                    for tb in range(NTB):
                        b = nbank()
                        for fl in range(len(fcs)):
                            mm(ps[:, b, :], wt[:, fl * 1024 + dm * 128:fl * 1024 + dm * 128 + 128],
                               actT[:, fl * 4 + tb, :], fl == 0, fl == len(fcs) - 1, [wkey, ("RA", fl * 4 + tb)], [pk(b)])
                        tt(xT[:, dm, cols(tb)], xT[:, dm, cols(tb)], ps[:, b, :], ALU.add,
                           [xk(dm, tb)], [pk(b), xk(dm, tb)])

        for sq_ in range(nseq):
            for c in range(KC):
                dma("sp", xT[:, c, :], x_d[sq_, c, :, :], [], [xk(c, tb) for tb in range(NTB)], c_x)
            for l in range(nlayers):
                wstate["layer"] = l
                wstate["off"] = 0
                if l % 2 == 0:
                    nsa_mixer(l, l // 2)
                else:
                    swa_mixer(l, l // 2)
                if STAGE >= 10:
                    ffn(l)
            if final_norm:
                def fo(c, tb, rstd, g, sq_=sq_):
                    oi = (c + tb) % 2
                    o = tf[:, 4 + oi, 0:512]
                    stt(o, xT[:, c, cols(tb)], g, rstd, ALU.mult, ALU.mult,
                        [xk(c, tb), tfk(3), "prm"], [tfk(4 + oi)])
                    dma("sp", y_d[sq_, c, :, cols(tb)], o, [tfk(4 + oi)], [("y", sq_, c, tb)], c_out[oi])
                norm(PRM.GFIN, fo)
            else:
                for c in range(KC):
                    dma("sp", y_d[sq_, c, :, :], xT[:, c, :], [xk(c, tb) for tb in range(NTB)],
                        [("y", sq_, c, tb) for tb in range(NTB)], c_out[c % 2])
        P.op("sp", None, [("y", s_, c, tb) for s_ in range(nseq) for c in range(KC) for tb in range(NTB)], [])

        P.finalize()
        with nc.Block() as block:
            @block.tensor
            def _(e):
                P.emit_engine("pe", e)

            @block.scalar
            def _(e):
                P.emit_engine("act", e)

            @block.vector
            def _(e):
                P.emit_engine("dve", e)

            @block.gpsimd
            def _(e):
                P.emit_engine("pool", e)

            @block.sync
            def _(e):
                P.emit_engine("sp", e)
    return nc


def _prep(inp):
    inp = {k: np.asarray(v, dtype=np.float32) for k, v in inp.items()}
    streams = []
    for l in range(DEPTH):
        j = l // 2
        if l % 2 == 0:
            tiles = _nsa_stream(inp["nsa_w_in"][j], inp["nsa_cmp_w1"][j], inp["nsa_w_o"][j],
                                inp["ffn_w_gu"][l], inp["ffn_w_down"][l])
        else:
            tiles = _swa_stream(inp["swa_w_qkv"][j], inp["swa_w_o"][j], inp["ffn_w_gu"][l], inp["ffn_w_down"][l])
        streams.append(np.ascontiguousarray(np.concatenate(tiles, axis=1)))
    cosT, sinT, cbf, cf = _consts()
    shared = {"prm": _params(inp), "cosT": cosT, "sinT": sinT, "cbf": cbf, "cf": cf,
              "nsasm": np.stack([_nsa_small(inp["nsa_cmp_w2"][j], inp["nsa_cmp_pe"][j]) for j in range(2)])}
    for l in range(DEPTH):
        shared["w%d" % l] = streams[l]
    return inp, shared


def kernel(**inputs):
    inp, shared = _prep(inputs)
    x = inp["x"]
    nc = build_program([shared["w%d" % l].shape[1] for l in range(DEPTH)])
    in_maps = []
    for core in range(8):
        xs = x[2 * core:2 * core + 2]
        xT = np.ascontiguousarray(xs.transpose(0, 2, 1)).reshape(2, KC, 128, S)
        m = dict(shared)
        m["xT"] = xT
        in_maps.append(m)
    res = run_bass_kernel_spmd(nc, in_maps, core_ids=list(range(8)))
    out = np.empty((16, S, D), np.float32)
    for core in range(8):
        yT = res.results[core]["yT"].reshape(2, D, S)
        out[2 * core:2 * core + 2] = yT.transpose(0, 2, 1)
    return out
```
